# Optimizing a Trainium2 kernel written in Bass

```python
import jax, jax.numpy as jnp
from jax import lax
import numpy as np

D_MODEL = 1024
BATCH = 8
SEQ = 4096
DEPTH = 1
DEC_BATCH = 2
DEC_SEQ = 8192
PAST_LEN = 128

DN_HEADS = 8
DN_DK = 128
DN_DV = 128
DN_CONV = 5
DN_CHUNK = 64
AT_HEADS = 8
AT_KV_HEADS = 2
AT_HEAD_DIM = 128
AT_WINDOW = 128
AT_BLOCK = 128
ROPE_THETA = 500000.0
ROPE_DIM = AT_HEAD_DIM // 4
D_FF = 2816
EPS = 1e-6
N_MOD = 9

DN_QK_W = DN_HEADS * DN_DK
DN_V_W = DN_HEADS * DN_DV
DN_QKV_W = 2 * DN_QK_W + DN_V_W
AT_Q_W = AT_HEADS * AT_HEAD_DIM
AT_KV_W = AT_KV_HEADS * AT_HEAD_DIM
MIX_SPLITS = (DN_QKV_W, DN_V_W, 2 * DN_HEADS, 2 * DN_HEADS, AT_Q_W, AT_KV_W, AT_KV_W, D_MODEL, D_MODEL)
PROJ_W = DN_QKV_W + DN_V_W + 4 * DN_HEADS + AT_Q_W + 2 * AT_KV_W + 2 * D_MODEL

kernel_name = "hybrid_deltanet_swa_macaron_encoder"


def rmsnorm(x, w):
    xf = x.astype(jnp.float32)
    y = xf * lax.rsqrt(jnp.mean(xf * xf, axis=-1, keepdims=True) + EPS)
    return (y * w.astype(jnp.float32)).astype(x.dtype)


def l2norm(x):
    xf = x.astype(jnp.float32)
    return xf * lax.rsqrt(jnp.sum(xf * xf, axis=-1, keepdims=True) + EPS)


def swiglu(x, w_in, w_out):
    g, u = jnp.split(x @ w_in, 2, axis=-1)
    return (jax.nn.silu(g) * u) @ w_out


def partial_rope(x, pos):
    half = ROPE_DIM // 2
    inv = jnp.power(jnp.float32(ROPE_THETA), -jnp.arange(half, dtype=jnp.float32) / half)
    ang = pos.astype(jnp.float32)[:, None] * inv[None, :]
    cos = jnp.cos(ang)[None, :, None, :]
    sin = jnp.sin(ang)[None, :, None, :]
    xr = x[..., :ROPE_DIM].astype(jnp.float32)
    x1, x2 = xr[..., :half], xr[..., half:]
    rot = jnp.concatenate([x1 * cos - x2 * sin, x2 * cos + x1 * sin], axis=-1).astype(x.dtype)
    return jnp.concatenate([rot, x[..., ROPE_DIM:]], axis=-1)


def centred_depthwise_conv(x, w):
    C = x.shape[-1]
    return lax.conv_general_dilated(
        x, w[:, None, :].astype(x.dtype), window_strides=(1,),
        padding=[(DN_CONV // 2, DN_CONV // 2)],
        dimension_numbers=('NWC', 'WIO', 'NWC'), feature_group_count=C)


def gated_delta_chunked(q, k, v, beta, g):
    B, T, H, dk = q.shape
    dv = v.shape[-1]
    C = DN_CHUNK
    N = T // C

    def chunks(a):
        a = a.reshape((B, N, C, H) + a.shape[3:])
        perm = (1, 0, 3, 2) + tuple(range(4, a.ndim))
        return a.transpose(perm)

    q = chunks(q * (dk ** -0.5))
    k = chunks(k)
    v = chunks(v)
    beta = chunks(beta)
    gc = jnp.cumsum(chunks(g), axis=-1)
    tri = jnp.tril(jnp.ones((C, C), dtype=bool))
    strict = jnp.tril(jnp.ones((C, C), dtype=bool), -1)
    decay = jnp.exp(jnp.where(tri, gc[..., :, None] - gc[..., None, :], -jnp.inf))
    kb = k * beta[..., None]
    vb = v * beta[..., None]
    L = jnp.where(strict, jnp.einsum('nbhid,nbhjd->nbhij', kb, k) * decay, 0.0)
    tmat = L + jnp.eye(C, dtype=jnp.float32)
    rhs = jnp.concatenate([vb, kb * jnp.exp(gc)[..., None]], axis=-1)
    sol = lax.linalg.triangular_solve(tmat, rhs, left_side=True, lower=True, unit_diagonal=True)
    u0, w = sol[..., :dv], sol[..., dv:]
    qk = jnp.einsum('nbhid,nbhjd->nbhij', q, k) * decay
    q_dec = q * jnp.exp(gc)[..., None]
    k_dec = k * jnp.exp(gc[..., -1:] - gc)[..., None]
    g_last = jnp.exp(gc[..., -1])

    def step(S, xs):
        qk_i, q_i, w_i, u_i, k_i, gl = xs
        u = u_i - jnp.einsum('bhcd,bhde->bhce', w_i, S)
        o = jnp.einsum('bhcd,bhde->bhce', q_i, S) + jnp.einsum('bhij,bhje->bhie', qk_i, u)
        S = S * gl[..., None, None] + jnp.einsum('bhcd,bhce->bhde', k_i, u)
        return S, o

    S0 = jnp.zeros((B, H, dk, dv), jnp.float32)
    _, o = lax.scan(step, S0, (qk, q_dec, w, u0, k_dec, g_last))
    return o.transpose(1, 0, 3, 2, 4).reshape(B, T, H, dv)


def deltanet_branch(qkv, z, b_raw, a_raw, conv_w, a_log, dt_bias, norm_w):
    B, T, _ = qkv.shape
    qkv = jax.nn.silu(centred_depthwise_conv(qkv, conv_w))
    q, k, v = jnp.split(qkv, [DN_QK_W, 2 * DN_QK_W], axis=-1)
    q = l2norm(q.reshape(B, T, DN_HEADS, DN_DK))
    k = l2norm(k.reshape(B, T, DN_HEADS, DN_DK))
    v = v.reshape(B, T, DN_HEADS, DN_DV).astype(jnp.float32)
    beta = jax.nn.sigmoid(b_raw.astype(jnp.float32)).reshape(B, T, 2, DN_HEADS)
    g = -jnp.exp(a_log.astype(jnp.float32)) * jax.nn.softplus(
        a_raw.astype(jnp.float32).reshape(B, T, 2, DN_HEADS) + dt_bias.astype(jnp.float32))
    o_f = gated_delta_chunked(q, k, v, beta[:, :, 0], g[:, :, 0])
    rev = lambda a: jnp.flip(a, axis=1)
    o_b = rev(gated_delta_chunked(rev(q), rev(k), rev(v), rev(beta[:, :, 1]), rev(g[:, :, 1])))
    o = rmsnorm(o_f + o_b, norm_w) * jax.nn.silu(z.reshape(B, T, DN_HEADS, DN_DV).astype(jnp.float32))
    return o.reshape(B, T, DN_V_W).astype(qkv.dtype)


def window_attention(q, k, v, sink):
    B, T, H, hd = q.shape
    Hkv = k.shape[2]
    G = H // Hkv
    Wb = AT_BLOCK
    W = AT_WINDOW
    nb = T // Wb

    def band(a):
        ap = jnp.pad(a, ((0, 0), (W, W), (0, 0), (0, 0))).reshape(B, nb + 2, Wb, Hkv, hd)
        return jnp.concatenate([ap[:, :-2], ap[:, 1:-1], ap[:, 2:]], axis=2)

    kw = band(k)
    vw = band(v)
    qb = q.reshape(B, nb, Wb, Hkv, G, hd)
    s = jnp.einsum('bnqkgd,bnskd->bnkgqs', qb, kw).astype(jnp.float32) * (hd ** -0.5)
    qi = jnp.arange(Wb)[:, None]
    sj = jnp.arange(3 * Wb)[None, :]
    inband = jnp.abs(sj - W - qi) <= W
    kpos = jnp.arange(nb)[:, None, None] * Wb + sj[None] - W
    valid = inband[None] & (kpos >= 0) & (kpos < T)
    s = jnp.where(valid[None, :, None, None], s, -jnp.inf)
    sink_l = sink.astype(jnp.float32).reshape(Hkv, G)[None, None, :, :, None, None]
    m = jnp.maximum(jnp.max(s, axis=-1, keepdims=True), sink_l)
    p = jnp.exp(s - m)
    p = p / (jnp.sum(p, axis=-1, keepdims=True) + jnp.exp(sink_l - m))
    o = jnp.einsum('bnkgqs,bnskd->bnqkgd', p.astype(v.dtype), vw)
    return o.reshape(B, T, H * hd)


def token_mixer(h, w_in, conv_w, a_log, dt_bias, dn_norm, attn_sink, w_proj_a, w_proj_b, w_out):
    B, T, _ = h.shape
    offs = [int(o) for o in np.cumsum(MIX_SPLITS)[:-1]]
    dn_qkv, dn_z, dn_b, dn_a, at_q, at_k, at_v, gate_a, gate_b = jnp.split(h @ w_in, offs, axis=-1)
    y_a = deltanet_branch(dn_qkv, dn_z, dn_b, dn_a, conv_w, a_log, dt_bias, dn_norm) @ w_proj_a
    pos = jnp.arange(T)
    q = partial_rope(at_q.reshape(B, T, AT_HEADS, AT_HEAD_DIM), pos)
    k = partial_rope(at_k.reshape(B, T, AT_KV_HEADS, AT_HEAD_DIM), pos)
    v = at_v.reshape(B, T, AT_KV_HEADS, AT_HEAD_DIM)
    y_b = window_attention(q, k, v, attn_sink) @ w_proj_b
    merged = jax.nn.sigmoid(gate_a) * y_a + jax.nn.sigmoid(gate_b) * y_b
    return merged @ w_out


def ada_rms(h, w, shift, scale):
    return rmsnorm(h, w) * (1 + scale) + shift


def encoder_layer(x, c, w_ada, b_ada, ffn1_norm, ffn1_w_in, ffn1_w_out, mix_norm, w_in, conv_w,
                  a_log, dt_bias, dn_norm, attn_sink, w_proj_a, w_proj_b, w_out,
                  ffn2_norm, ffn2_w_in, ffn2_w_out):
    mod = (jax.nn.silu(c) @ w_ada + b_ada)[:, None, :]
    sh1, sc1, gt1, sh2, sc2, gt2, sh3, sc3, gt3 = jnp.split(mod, N_MOD, axis=-1)
    x = x + 0.5 * gt1 * swiglu(ada_rms(x, ffn1_norm, sh1, sc1), ffn1_w_in, ffn1_w_out)
    x = x + gt2 * token_mixer(ada_rms(x, mix_norm, sh2, sc2), w_in, conv_w, a_log, dt_bias,
                              dn_norm, attn_sink, w_proj_a, w_proj_b, w_out)
    x = x + 0.5 * gt3 * swiglu(ada_rms(x, ffn2_norm, sh3, sc3), ffn2_w_in, ffn2_w_out)
    return x


def setup_inputs(seed: int = 0) -> dict:
    key = jax.random.key(seed)
    ks = jax.random.split(key, 24)
    f32 = jnp.float32
    D = D_MODEL
    nrm = lambda k, shape, s: jax.random.normal(k, shape, f32) * s
    gain = lambda k, shape: 1.0 + 0.02 * jax.random.normal(k, shape, f32)
    dt = jnp.exp(jax.random.uniform(ks[12], (DEPTH, 2, DN_HEADS), f32, np.log(1e-3), np.log(1e-1)))
    return {
        "x_prompt": nrm(ks[0], (BATCH, SEQ, D), 1.0),
        "x_sample": nrm(ks[1], (DEC_BATCH, DEC_SEQ, D), 1.0),
        "c_prompt": nrm(ks[2], (BATCH, D), 1.0),
        "c_sample": nrm(ks[3], (DEC_BATCH, D), 1.0),
        "w_ada": nrm(ks[4], (DEPTH, D, N_MOD * D), 0.5 * D ** -0.5),
        "b_ada": nrm(ks[5], (DEPTH, N_MOD * D), 0.02),
        "ffn1_norm": gain(ks[6], (DEPTH, D)),
        "ffn1_w_in": nrm(ks[7], (DEPTH, D, 2 * D_FF), D ** -0.5),
        "ffn1_w_out": nrm(ks[8], (DEPTH, D_FF, D), D_FF ** -0.5),
        "mix_norm": gain(ks[9], (DEPTH, D)),
        "w_in": nrm(ks[10], (DEPTH, D, PROJ_W), D ** -0.5),
        "conv_w": nrm(ks[11], (DEPTH, DN_CONV, DN_QKV_W), DN_CONV ** -0.5),
        "a_log": jnp.log(jax.random.uniform(ks[13], (DEPTH, 2, DN_HEADS), f32, 1.0, 16.0)),
        "dt_bias": dt + jnp.log(-jnp.expm1(-dt)),
        "dn_norm": gain(ks[14], (DEPTH, DN_DV)),
        "attn_sink": nrm(ks[15], (DEPTH, AT_HEADS), 1.0),
        "w_proj_a": nrm(ks[16], (DEPTH, DN_V_W, D), DN_V_W ** -0.5),
        "w_proj_b": nrm(ks[17], (DEPTH, AT_Q_W, D), AT_Q_W ** -0.5),
        "w_out": nrm(ks[18], (DEPTH, D, D), D ** -0.5),
        "ffn2_norm": gain(ks[19], (DEPTH, D)),
        "ffn2_w_in": nrm(ks[20], (DEPTH, D, 2 * D_FF), D ** -0.5),
        "ffn2_w_out": nrm(ks[21], (DEPTH, D_FF, D), D_FF ** -0.5),
        "final_norm": gain(ks[22], (D,)),
    }


def reference(x_prompt, x_sample, c_prompt, c_sample, w_ada, b_ada, ffn1_norm, ffn1_w_in, ffn1_w_out,
              mix_norm, w_in, conv_w, a_log, dt_bias, dn_norm, attn_sink, w_proj_a, w_proj_b, w_out,
              ffn2_norm, ffn2_w_in, ffn2_w_out, final_norm):
    def trunk(x, c):
        for l in range(DEPTH):
            x = encoder_layer(x, c, w_ada[l], b_ada[l], ffn1_norm[l], ffn1_w_in[l], ffn1_w_out[l],
                              mix_norm[l], w_in[l], conv_w[l], a_log[l], dt_bias[l], dn_norm[l],
                              attn_sink[l], w_proj_a[l], w_proj_b[l], w_out[l],
                              ffn2_norm[l], ffn2_w_in[l], ffn2_w_out[l])
        return rmsnorm(x, final_norm)

    y_prompt = trunk(x_prompt, c_prompt)
    y_sample = trunk(x_sample, c_sample)
    return (y_prompt, y_sample)
```

```python
import contextlib
import numpy as np
import concourse.bass as bass
import concourse.mybir as mybir
from concourse.bass_utils import run_bass_kernel_spmd

F32 = mybir.dt.float32
BF16 = mybir.dt.bfloat16
AF = mybir.ActivationFunctionType
ALU = mybir.AluOpType

NTOK = 8192
SLOT = 4096
D = 1024
T = 512
NT = NTOK // T
FF = 2816
PW = 7712
EPS = 1e-6
NEG = -30000.0
O_Z, O_B, O_Q, O_K, O_V, O_GA = 3072, 4096, 4128, 5152, 5408, 5664


class Buf:
    __slots__ = ("name", "lw", "rd", "excl")

    def __init__(self, name, excl=False):
        self.name = name
        self.lw = None
        self.rd = {}
        self.excl = excl


class Sched:
    ENGS = ("pe", "act", "dve", "pool", "sp")

    def __init__(self):
        self.ops = {e: [] for e in self.ENGS}
        self.dq = {}
        self.seen = {}

    def _stream(self, eng):
        if eng in self.ops:
            return self.ops[eng]
        return self.dq.setdefault(eng, [])

    def _deps(self, eng, reads, writes):
        need = {}

        same_ok = eng in ("act", "dve", "pool")

        def add(p, same=False):
            if p is None:
                return
            pe, pi = p
            if pe == eng and not (same and same_ok):
                return
            if need.get(pe, -1) < pi:
                need[pe] = pi

        for b in reads:
            add(b.lw, True)
        for b in writes:
            add(b.lw, True)
            for pe, pi in b.rd.items():
                add((pe, pi))
        return need

    def _commit(self, eng, idx, reads, writes):
        for b in reads:
            b.rd[eng] = idx
        for b in writes:
            b.lw = (eng, idx)
            b.rd = {}

    def _filter(self, cons, need):
        out = {}
        for pe, pi in need.items():
            k = (cons, pe)
            if self.seen.get(k, -1) >= pi:
                continue
            self.seen[k] = pi
            out[pe] = pi
            self._stream(pe)[pi][2] = True
        return out

    def op(self, eng, fn, reads=(), writes=()):
        if any(b.excl for b in reads):
            writes = list(writes) + [b for b in reads if b.excl]
            reads = [b for b in reads if not b.excl]
        need = self._filter(eng, self._deps(eng, reads, writes))
        lst = self.ops[eng]
        idx = len(lst)
        lst.append([fn, need, False, None])
        self._commit(eng, idx, reads, writes)
        return idx

    def dma(self, issuer, cls, fn, reads=(), writes=()):
        rd_dram = bool(reads) and reads[0].name.startswith(("scr_", "inputs"))
        if rd_dram and writes:
            cls = "L_" + writes[0].name
        elif reads:
            cls = "S_" + reads[0].name
        need = self._deps(cls, reads, writes)
        need = {pe: pi for pe, pi in need.items() if pe != issuer}
        need = self._filter(issuer, need)
        q = self._stream(cls)
        didx = len(q)
        q.append([None, {}, True, None])
        self.ops[issuer].append([fn, need, False, (cls, didx)])
        self._commit(cls, didx, reads, writes)
        return didx

    def barrier(self):
        last = {}
        for e in self.ENGS:
            lst = self.ops[e]
            for i in range(len(lst) - 1, -1, -1):
                if lst[i][0] is not None and lst[i][3] is None:
                    last[e] = i
                    break
        for c, q in self.dq.items():
            if q:
                last[c] = len(q) - 1
        for e in self.ENGS:
            need = {p: i for p, i in last.items() if p != e}
            need = self._filter(e, need)
            self.ops[e].append([None, need, False, None])

    def emit(self, nc, sems, mk_sem=None):
        for c in self.dq:
            if c not in sems:
                sems[c] = mk_sem(c)
        num = {}
        for e, lst in list(self.ops.items()) + list(self.dq.items()):
            c = 0
            arr = []
            for o in lst:
                if o[2]:
                    c += 1
                arr.append(c)
            num[e] = arr
        handles = {"pe": "tensor", "act": "scalar", "dve": "vector", "pool": "gpsimd", "sp": "sync"}
        stats = {}
        with nc.Block() as block:
            for e in self.ENGS:
                lst = self.ops[e]
                stats[e] = len(lst)
                if not lst:
                    continue

                def body(engh, e=e, lst=lst):
                    for o in lst:
                        fn, need, needed, dmaref = o
                        for pe, pi in need.items():
                            mult = 16 if pe in self.dq else 1
                            engh.wait_ge(sems[pe], num[pe][pi] * mult)
                        if fn is None:
                            continue
                        ins = fn(engh)
                        if dmaref is not None:
                            ins.then_inc(sems[dmaref[0]], 16)
                        elif needed:
                            ins.then_inc(sems[e], 1)

                getattr(block, handles[e])(body)
        return stats


class Ring:
    def __init__(self, aps, name):
        self.items = [(a, Buf("%s%d" % (name, i))) for i, a in enumerate(aps)]
        self.i = 0

    def next(self):
        it = self.items[self.i % len(self.items)]
        self.i += 1
        return it


class Arena:
    def __init__(self, ap, n):
        self.ap = ap
        self.n = n
        self.off = 0

    def reset(self, off=0):
        self.off = off

    def alloc(self, free, dt):
        free = tuple(free)
        cnt = int(np.prod(free))
        ne = cnt * (2 if dt == F32 else 1)
        assert self.off + ne <= self.n, ("arena overflow", self.off, ne, self.n)
        a = self.ap[:, self.off:self.off + ne]
        self.off += (ne + 31) // 32 * 32
        if dt == F32:
            a = a.bitcast(F32)
        if len(free) == 2:
            a = a.rearrange("p (a b) -> p a b", a=free[0])
        elif len(free) == 3:
            a = a.rearrange("p (a b c) -> p a b c", a=free[0], b=free[1])
        return a


def MM(out, lhsT, rhs, start=True, stop=True):
    return lambda e: e.matmul(out, lhsT=lhsT, rhs=rhs, start=start, stop=stop)


def TR(out, in_, ident):
    return lambda e: e.transpose(out=out, in_=in_, identity=ident)


def ACTV(out, in_, func, bias=None, scale=None, accum_out=None):
    kw = {}
    if bias is not None:
        kw["bias"] = bias
    if scale is not None:
        kw["scale"] = scale
    if accum_out is not None:
        kw["accum_out"] = accum_out
    return lambda e: e.activation(out=out, in_=in_, func=func, **kw)


def TT(out, in0, in1, op):
    return lambda e: e.tensor_tensor(out=out, in0=in0, in1=in1, op=op)


def TS(out, in0, s1, s2, op0, op1=None):
    if op1 is None:
        return lambda e: e.tensor_scalar(out=out, in0=in0, scalar1=s1, scalar2=None, op0=op0)
    return lambda e: e.tensor_scalar(out=out, in0=in0, scalar1=s1, scalar2=s2, op0=op0, op1=op1)


def STT(out, in0, scalar, in1, op0, op1):
    return lambda e: e.scalar_tensor_tensor(out=out, in0=in0, scalar=scalar, in1=in1, op0=op0, op1=op1)


def CP(out, in_):
    return lambda e: e.tensor_copy(out=out, in_=in_)


def MSET(out, v):
    return lambda e: e.memset(out, v)


def DMA(out, in_):
    return lambda e: e.dma_start(out=out, in_=in_)


WEIGHTS = [("ffn1_w_in", D, 2 * FF), ("ffn1_w_out", FF, D), ("w_in", D, PW), ("w_proj_a", D, D),
           ("w_proj_b", D, D), ("w_out", D, D), ("ffn2_w_in", D, 2 * FF), ("ffn2_w_out", FF, D)]


def build(dev=0):
    nc = bass.Bass("TRN2", target_bir_lowering=False)
    S = Sched()

    def din(name, shape, dt=F32):
        return nc.dram_tensor(name, list(shape), dt, kind="ExternalInput").ap()

    def dscr(name, shape, dt):
        return nc.dram_tensor(name, list(shape), dt, kind=("ExternalOutput" if dev else "Internal")).ap()

    x_d = din("x", [NTOK, D])
    cT_d = din("cT", [128, 8, 2])
    link_d = din("link", [128, 1])
    cos_d = din("cosT", [128, NTOK])
    sin_d = din("sinT", [128, NTOK])
    wada_d = din("w_ada", [D, 9 * D])
    bada_d = din("b_adaT", [128, 72])
    nrm_d = din("nrmT", [128, 4, 8])
    convw_d = din("conv_wT", [128, 24, 5])
    alog_d = din("a_log_bc", [128, 64])
    dtb_d = din("dt_bias_bc", [128, 64])
    hmask_d = din("hmask", [128, 2])
    dnn_d = din("dn_norm_bc", [128, 128])
    sink_d = din("sink_bc", [128, 8])
    identf_d = din("identf", [128, 128])
    dmask_d = din("dmask", [128, 6, 128])
    tri_d = din("tri", [128, 3, 128])
    amask_d = din("amask", [128, 2, 512])
    perm_d = din("perm32", [128, 128])
    wsrc = {n: din(n, [k, m]) for n, k, m in WEIGHTS}
    y_d = nc.dram_tensor("y", [NTOK, D], F32, kind="ExternalOutput").ap()

    wb = {n: nc.dram_tensor(n + "_b", [k, m], BF16, kind="Internal").ap() for n, k, m in WEIGHTS}
    x1T_d = dscr("x1T", [D, NTOK], F32)
    qkvT_d = dscr("qkvT", [3072, NTOK], BF16)
    zs_d = dscr("zs", [NTOK, D], BF16)
    ba_d = dscr("ba", [NTOK, 32], F32)
    aqT_d = dscr("aqT", [D, NTOK], BF16)
    akT_d = dscr("akT", [256, NTOK], BF16)
    avs_d = dscr("avs", [NTOK, 256], BF16)
    gT_d = dscr("gT", [2048, NTOK], BF16)
    obs_d = dscr("obs", [NTOK, D], F32)
    dnT_d = dscr("dnT", [D, NTOK], BF16)
    atT_d = dscr("atT", [D, NTOK], BF16)
    modT_o = dscr("modT_o", [128, 144], F32) if dev else None
    hdbg = dscr("hdbg", [128, 8 * 512], BF16) if dev else None
    hiddbg = dscr("hiddbg", [128, 24 * 512], BF16) if dev else None
    ofs_d = dscr("ofs", [NTOK, D], F32) if dev else None
    gdbg = dscr("gdbg", [128, 128], F32) if dev else None

    AR_N = 93 * 1024
    with contextlib.ExitStack() as es:
        def sbt(name, shape, dt):
            return es.enter_context(nc.sbuf_tensor(name, list(shape), dt))

        arena_t = sbt("arena", [128, AR_N], BF16)
        ar = Arena(arena_t[:, :], AR_N)
        identf = sbt("identf_s", [128, 128], F32)
        identb = sbt("identb_s", [128, 128], BF16)
        onesb = sbt("onesb_s", [128, 128], BF16)
        link = sbt("link_s", [128, 1], F32)
        epsb = sbt("epsb_s", [128, 1], F32)
        modT = sbt("modT_s", [128, 72, 2], F32)
        A_t = sbt("A_s", [128, 3, 2, 8], F32)
        G_t = sbt("G_s", [128, 3, 2, 8], F32)
        AF_t = sbt("AF_s", [128, 8], F32)
        nrmT = sbt("nrmT_s", [128, 4, 8], F32)
        cT = sbt("cT_s", [128, 8, 2], F32)
        badaT = sbt("badaT_s", [128, 72], F32)
        psum = [es.enter_context(nc.psum_tensor("ps%d" % i, [128, 512], F32)) for i in range(8)]
        semn = ["pe", "act", "dve", "pool", "sp", "sp_ld", "sp_ldw", "sp_st"]
        sems = {n: es.enter_context(nc.semaphore(n)) for n in semn}
        bconst = Buf("const")
        bmod = Buf("mod")
        bscr = {n: Buf("scr_" + n) for n in ["wb", "x1T", "qkvT", "zs", "ba", "aqT", "akT", "avs", "gT", "obs", "dnT", "atT", "y", "dbg"]}
        bin_ = Buf("inputs")

        def psring():
            return Ring([p[:, :] for p in psum], "ps")

        S.dma("sp", "sp_ld", DMA(identf[:, :], identf_d), reads=[bin_], writes=[bconst])
        S.dma("sp", "sp_ld", DMA(link[:, :], link_d), reads=[bin_], writes=[bconst])
        S.dma("sp", "sp_ld", DMA(nrmT[:, :, :], nrm_d), reads=[bin_], writes=[bconst])
        S.dma("sp", "sp_ld", DMA(cT[:, :, :], cT_d), reads=[bin_], writes=[bconst])
        S.dma("sp", "sp_ld", DMA(badaT[:, :], bada_d), reads=[bin_], writes=[bconst])
        S.op("dve", CP(identb[:, :], identf[:, :]), reads=[bconst], writes=[bconst])
        S.op("dve", MSET(onesb[:, :], 1.0), writes=[bconst])
        S.op("dve", MSET(epsb[:, :], 1024.0 * EPS), writes=[bconst])

        ar.reset()
        stg = Ring([ar.alloc((2816,), F32) for _ in range(3)], "stg")
        cvt = Ring([ar.alloc((2816,), BF16) for _ in range(3)], "cvt")
        engs = ["act", "dve"]
        ei = 0
        for n, K, N in (WEIGHTS if dev != 12 else []):
            for r in range(K // 128):
                for c0 in range(0, N, 2816):
                    cw = min(2816, N - c0)
                    st, stb = stg.next()
                    cv, cvb = cvt.next()
                    S.dma("sp", "sp_ldw", DMA(st[:, :cw], wsrc[n][r * 128:(r + 1) * 128, c0:c0 + cw]), reads=[bin_], writes=[stb])
                    e = engs[ei % 2]
                    ei += 1
                    if e == "act":
                        S.op("act", ACTV(cv[:, :cw], st[:, :cw], AF.Copy), reads=[stb], writes=[cvb])
                    else:
                        S.op(e, CP(cv[:, :cw], st[:, :cw]), reads=[stb], writes=[cvb])
                    S.dma("sp", "sp_st", DMA(wb[n][r * 128:(r + 1) * 128, c0:c0 + cw], cv[:, :cw]), reads=[cvb], writes=[bscr["wb"]])

        scT = ar.alloc((8, 2), F32)
        bsc = Buf("scT")
        S.op("act", ACTV(scT, cT[:, :, :], AF.Silu), reads=[bconst], writes=[bsc])
        wring = Ring([ar.alloc((8, 1152), F32) for _ in range(2)], "wada")
        pm = psum[0][:, 0:144]
        pmb = Buf("pm")
        wada_v = wada_d.rearrange("(kc p) n -> p kc n", p=128)
        for jb in (range(8) if dev != 13 else []):
            wt, wtb = wring.next()
            S.dma("sp", "sp_ldw", DMA(wt, wada_v[:, :, jb * 1152:(jb + 1) * 1152]), reads=[bin_], writes=[wtb])
            for jj in range(9):
                j = jb * 9 + jj
                for kc in range(8):
                    S.op("pe", MM(pm[:, 2 * j:2 * j + 2], wt[:, kc, jj * 128:(jj + 1) * 128], scT[:, kc, :], kc == 0, kc == 7),
                         reads=[wtb, bsc], writes=[pmb])
        pm3 = pm.rearrange("p (j s) -> p j s", s=2)
        for s in range(2):
            S.op("dve", TT(modT[:, :, s], pm3[:, :, s], badaT[:, :], ALU.add), reads=[pmb, bconst], writes=[bmod])
        for n in range(3):
            for s in range(2):
                S.op("dve", STT(A_t[:, n, s, :], modT[:, (3 * n + 1) * 8:(3 * n + 1) * 8 + 8, s], 1.0, nrmT[:, n, :], ALU.add, ALU.mult),
                     reads=[bmod, bconst], writes=[bmod])
                S.op("dve", TS(A_t[:, n, s, :], A_t[:, n, s, :], 32.0, None, ALU.mult), reads=[bmod], writes=[bmod])
                S.op("dve", TS(G_t[:, n, s, :], modT[:, (3 * n + 2) * 8:(3 * n + 2) * 8 + 8, s], (1.0 if n == 1 else 0.5), None, ALU.mult),
                     reads=[bmod], writes=[bmod])
        S.op("dve", TS(AF_t[:, :], nrmT[:, 3, :], 32.0, None, ALU.mult), reads=[bconst], writes=[bmod])
        if dev:
            S.dma("sp", "sp_st", DMA(modT_o, modT[:, :, :].rearrange("p j s -> p (j s)")), reads=[bmod], writes=[bscr["dbg"]])
        S.barrier()

        def finish():
            S.dma("sp", "sp_st", DMA(y_d[0:128, 0:128], identf[:, :]), reads=[bconst], writes=[bscr["y"]])
            S.barrier()
            st = S.emit(nc, sems, lambda n: es.enter_context(nc.semaphore(n)))
            print("instr counts", st)
            return nc

        if dev in (11, 12, 13):
            return finish()

        def rms_ada(xT, xb, n, s, hout, hb, rings, out_f32=False):
            sqr, rsr, tmr, psr = rings
            ps, psb = psr.next()
            for kc in range(8):
                sq, sqb = sqr.next()
                S.op("act", ACTV(sq, xT[:, kc, :], AF.Square), reads=[xb], writes=[sqb])
                S.op("pe", MM(ps, onesb[:, :], sq, kc == 0, kc == 7), reads=[sqb, bconst], writes=[psb])
            rs, rsb = rsr.next()
            S.op("act", ACTV(rs, ps, AF.Sqrt, bias=epsb[:, 0:1], scale=1.0), reads=[psb, bconst], writes=[rsb])
            S.op("dve", lambda e, rs=rs: e.reciprocal(out=rs, in_=rs), reads=[rsb], writes=[rsb])
            for kc in range(8):
                if n == 3:
                    S.op("dve", STT(hout[:, kc, :], xT[:, kc, :], AF_t[:, kc:kc + 1], rs, ALU.mult, ALU.mult),
                         reads=[xb, rsb, bmod], writes=[hb])
                else:
                    tm, tmb = tmr.next()
                    S.op("dve", STT(tm, xT[:, kc, :], A_t[:, n, s, kc:kc + 1], rs, ALU.mult, ALU.mult),
                         reads=[xb, rsb, bmod], writes=[tmb])
                    S.op("act", ACTV(hout[:, kc, :], tm, AF.Identity, bias=modT[:, 3 * n * 8 + kc, s:s + 1], scale=1.0),
                         reads=[tmb, bmod], writes=[hb])

        def gemm_fm(Wd, KC, segs_per_group, rhs, rhs_bufs, epi, wr, psr):
            Wv = Wd.rearrange("(kc p) n -> p kc n", p=128)
            for g, segs in enumerate(segs_per_group):
                wt, wtb = wr.next()
                gw = sum(w for _, w in segs)
                wt = wt[:, 0:KC * gw].rearrange("p (kc n) -> p kc n", kc=KC)
                o = 0
                for c0, w in segs:
                    S.dma("sp", "sp_ldw", DMA(wt[:, :, o:o + w], Wv[:, :, c0:c0 + w]), reads=[bscr["wb"]], writes=[wtb])
                    o += w
                outs = []
                for ci in range(o // 128):
                    ps, psb = psr.next()
                    for kc in range(KC):
                        S.op("pe", MM(ps, wt[:, kc, ci * 128:(ci + 1) * 128], rhs(kc), kc == 0, kc == KC - 1),
                             reads=[wtb] + rhs_bufs, writes=[psb])
                    outs.append((ps, psb))
                epi(g, outs)

        def gemm_tm(Wd, KC, c0, ncols, lhsT, lhs_bufs, epi, wr, psr):
            Wv = Wd.rearrange("(kc p) n -> p kc n", p=128)
            wt, wtb = wr.next()
            wt = wt[:, 0:KC * ncols].rearrange("p (kc n) -> p kc n", kc=KC)
            S.dma("sp", "sp_ldw", DMA(wt, Wv[:, :, c0:c0 + ncols]), reads=[bscr["wb"]], writes=[wtb])
            for tb in range(4):
                ps, psb = psr.next()
                for kc in range(KC):
                    S.op("pe", MM(ps[:, 0:ncols], lhsT(kc, tb), wt[:, kc, :], kc == 0, kc == KC - 1),
                         reads=[wtb] + lhs_bufs, writes=[psb])
                epi(tb, ps[:, 0:ncols], psb)

        def swiglu_ffn(w_in_name, w_out_name, h, hb, hid, hidb, xT, xb, n, s, wr, psr, sgr):
            segs = [[(j * 128, 256), (FF + j * 128, 256)] for j in range(0, 22, 2)]

            def epi_in(g, outs):
                for q in range(2):
                    sg, sgb = sgr.next()
                    S.op("act", ACTV(sg, outs[q][0], AF.Silu), reads=[outs[q][1]], writes=[sgb])
                    S.op("dve", TT(hid[:, g * 2 + q, :], sg, outs[2 + q][0], ALU.mult), reads=[sgb, outs[2 + q][1]], writes=[hidb])

            gemm_fm(wb[w_in_name], 8, segs, lambda kc: h[:, kc, :], [hb], epi_in, wr, psr)
            segs2 = [[(fo * 256, 256)] for fo in range(4)]

            def epi_out(g, outs):
                for q in range(2):
                    fo = g * 2 + q
                    S.op("dve", STT(xT[:, fo, :], outs[q][0], G_t[:, n, s, fo:fo + 1], xT[:, fo, :], ALU.mult, ALU.add),
                         reads=[outs[q][1], bmod, xb], writes=[xb])

            gemm_fm(wb[w_out_name], 22, segs2, lambda j: hid[:, j, :], [hidb], epi_out, wr, psr)

        ar.reset()
        xT = ar.alloc((8, 512), F32); xTb = Buf("xT")
        h = ar.alloc((8, 512), BF16); hb = Buf("h")
        R1 = ar.alloc((24, 512), BF16); bR1 = Buf("R1")
        hid = R1
        R2 = ar.alloc((16, 512), BF16); bR2 = Buf("R2")
        xin = R2.rearrange("p a b -> p (a b)").bitcast(F32).rearrange("p (a b) -> p a b", a=4)
        aq_st = ar.alloc((10, 512), BF16); baq = Buf("aqst")
        z_st = ar.alloc((4, 1024), BF16); bz = Buf("zst")
        ba_st = ar.alloc((4, 32), F32); bba = Buf("bast")
        v_st = ar.alloc((4, 256), BF16); bv = Buf("vst")
        cs_t = ar.alloc((2, 512), F32); bcs = Buf("cs")
        xr_r = Ring([ar.alloc((512,), F32) for _ in range(2)], "xr")
        rt_r = Ring([ar.alloc((512,), F32) for _ in range(2)], "rt")
        sqr = Ring([ar.alloc((512,), BF16) for _ in range(2)], "sq")
        rsr = Ring([ar.alloc((512,), F32) for _ in range(2)], "rs")
        tmr = Ring([ar.alloc((512,), F32) for _ in range(3)], "tm")
        sgr = Ring([ar.alloc((512,), F32) for _ in range(3)], "sg")
        wr = Ring([ar.alloc((22 * 256,), BF16) for _ in range(3)], "w")
        perm = ar.alloc((128,), F32); bperm = Buf("perm")
        S.dma("sp", "sp_ld", DMA(perm[:, :], perm_d), reads=[bin_], writes=[bperm])
        psr = psring()
        rings = (sqr, rsr, tmr, psr)
        n_tiles = NT if dev not in (10, 14, 15, 16, 17, 18, 19) else 2
        for ti in range(n_tiles):
            t0 = ti * T
            s = ti // (NT // 2)
            S.dma("sp", "sp_ld", DMA(xin, x_d[t0:t0 + T, :].rearrange("(tb p) f -> p tb f", p=128)), reads=[bin_], writes=[bR2])
            for kc in range(8):
                ps, psb = psr.next()
                for tb in range(4):
                    S.op("pe", TR(ps[:, tb * 128:(tb + 1) * 128], xin[:, tb, kc * 128:(kc + 1) * 128], identf[:, :]),
                         reads=[bR2, bconst], writes=[psb])
                S.op(("act" if kc % 2 else "dve"), (ACTV(xT[:, kc, :], ps, AF.Copy) if kc % 2 else CP(xT[:, kc, :], ps)),
                     reads=[psb], writes=[xTb])
            if dev != 14:
                rms_ada(xT, xTb, 0, s, h, hb, rings)
            if dev not in (14, 15):
                swiglu_ffn("ffn1_w_in", "ffn1_w_out", h, hb, hid, bR1, xT, xTb, 0, s, wr, psr, sgr)
            S.dma("sp", "sp_st", DMA(x1T_d.rearrange("(kc p) t -> p kc t", p=128)[:, :, t0:t0 + T], xT), reads=[xTb], writes=[bscr["x1T"]])
            if dev == 16 and ti == 0:
                S.dma("sp", "sp_st", DMA(hdbg, h.rearrange("p a b -> p (a b)")), reads=[hb], writes=[bscr["dbg"]])
                S.dma("sp", "sp_st", DMA(hiddbg, R1.rearrange("p a b -> p (a b)")), reads=[bR1], writes=[bscr["dbg"]])
            if dev in (14, 15, 16):
                continue
            rms_ada(xT, xTb, 1, s, h, hb, rings)
            hk = lambda kc: h[:, kc, :]
            def epi_qkv(g, outs):
                for q, (ps, psb) in enumerate(outs):
                    c = g * 4 + q
                    if q % 2:
                        S.op("act", ACTV(R1[:, c, :], ps, AF.Copy), reads=[psb], writes=[bR1])
                    else:
                        S.op("dve", CP(R1[:, c, :], ps), reads=[psb], writes=[bR1])
            gemm_fm(wb["w_in"], 8, [[(g * 512, 512)] for g in range(6)], hk, [hb], epi_qkv, wr, psr)
            for c8 in range(3):
                S.dma("sp", "sp_st", DMA(qkvT_d.rearrange("(c p) t -> p c t", p=128)[:, c8 * 8:(c8 + 1) * 8, t0:t0 + T], R1[:, c8 * 8:(c8 + 1) * 8, :]), reads=[bR1], writes=[bscr["qkvT"]])
            if dev == 17:
                continue
            S.dma("sp", "sp_ld", DMA(cs_t[:, 0, :], cos_d[:, t0:t0 + T]), reads=[bin_], writes=[bcs])
            S.dma("sp", "sp_ld", DMA(cs_t[:, 1, :], sin_d[:, t0:t0 + T]), reads=[bin_], writes=[bcs])

            def epi_rope(base):
                def f(g, outs):
                    for q, (ps, psb) in enumerate(outs):
                        c = base + g * 4 + q
                        xr, xrb = xr_r.next()
                        S.op("dve", CP(xr[:, :], ps[:, :]), reads=[psb], writes=[xrb])
                        pp, ppb = psr.next()
                        S.op("pe", MM(pp[:, :], perm[:, :], xr[:, :]), reads=[bperm, xrb], writes=[ppb])
                        rt, rtb = rt_r.next()
                        S.op("dve", TT(rt[:, :], pp[:, :], cs_t[:, 1, :], ALU.mult), reads=[ppb, bcs], writes=[rtb])
                        S.op("dve", TT(xr[:, :], xr[:, :], cs_t[:, 0, :], ALU.mult), reads=[xrb, bcs], writes=[xrb])
                        S.op("dve", TT(aq_st[:, c, :], xr[:, :], rt[:, :], ALU.add), reads=[xrb, rtb, baq], writes=[baq])
                return f
            gemm_fm(wb["w_in"], 8, [[(O_Q + g * 512, 512)] for g in range(2)], hk, [hb], epi_rope(0), wr, psr)
            gemm_fm(wb["w_in"], 8, [[(O_K, 256)]], hk, [hb], epi_rope(8), wr, psr)
            S.dma("sp", "sp_st", DMA(aqT_d.rearrange("(c p) t -> p c t", p=128)[:, :, t0:t0 + T], aq_st[:, 0:8, :]), reads=[baq], writes=[bscr["aqT"]])
            S.dma("sp", "sp_st", DMA(akT_d.rearrange("(c p) t -> p c t", p=128)[:, :, t0:t0 + T], aq_st[:, 8:10, :]), reads=[baq], writes=[bscr["akT"]])
            if dev == 18:
                continue
            def epi_gate(g, outs):
                for q, (ps, psb) in enumerate(outs):
                    S.op("act", ACTV(R2[:, g * 4 + q, :], ps, AF.Sigmoid), reads=[psb], writes=[bR2])
            gemm_fm(wb["w_in"], 8, [[(O_GA + g * 512, 512)] for g in range(4)], hk, [hb], epi_gate, wr, psr)
            for c8 in range(2):
                S.dma("sp", "sp_st", DMA(gT_d.rearrange("(c p) t -> p c t", p=128)[:, c8 * 8:(c8 + 1) * 8, t0:t0 + T], R2[:, c8 * 8:(c8 + 1) * 8, :]), reads=[bR2], writes=[bscr["gT"]])
            if dev == 19:
                continue
            hl = lambda kc, tb: h[:, kc, tb * 128:(tb + 1) * 128]
            for zc in range(2):
                def epi_z(tb, ps, psb, zc=zc):
                    S.op("act", ACTV(z_st[:, tb, zc * 512:(zc + 1) * 512], ps, AF.Silu), reads=[psb], writes=[bz])
                gemm_tm(wb["w_in"], 8, O_Z + zc * 512, 512, hl, [hb], epi_z, wr, psr)
            S.dma("sp", "sp_st", DMA(zs_d[t0:t0 + T, :].rearrange("(tb p) f -> p tb f", p=128), z_st), reads=[bz], writes=[bscr["zs"]])

            def epi_ba(tb, ps, psb):
                S.op("dve", CP(ba_st[:, tb, :], ps), reads=[psb], writes=[bba])
            gemm_tm(wb["w_in"], 8, O_B, 32, hl, [hb], epi_ba, wr, psr)
            S.dma("sp", "sp_st", DMA(ba_d[t0:t0 + T, :].rearrange("(tb p) f -> p tb f", p=128), ba_st), reads=[bba], writes=[bscr["ba"]])

            def epi_v(tb, ps, psb):
                S.op("dve", CP(v_st[:, tb, :], ps), reads=[psb], writes=[bv])
            gemm_tm(wb["w_in"], 8, O_V, 256, hl, [hb], epi_v, wr, psr)
            S.dma("sp", "sp_st", DMA(avs_d[t0:t0 + T, :].rearrange("(tb p) f -> p tb f", p=128), v_st), reads=[bv], writes=[bscr["avs"]])
        S.barrier()

        if dev in (1, 10, 14, 15, 16, 17, 18, 19):
            return finish()

        ar.reset()
        es_t = ar.alloc((8,), F32); bes = Buf("es")
        sk_t = ar.alloc((8,), F32)
        am_t = ar.alloc((2, 512), F32); bam = Buf("am")
        S.dma("sp", "sp_ld", DMA(sk_t, sink_d), reads=[bin_], writes=[bes])
        S.dma("sp", "sp_ld", DMA(am_t, amask_d), reads=[bin_], writes=[bam])
        S.op("act", ACTV(es_t, sk_t, AF.Exp), reads=[bes], writes=[bes])
        q_r = Ring([ar.alloc((8, 512), BF16) for _ in range(2)], "aq")
        k_r = Ring([ar.alloc((2, 768), BF16) for _ in range(2)], "akk")
        v_r = Ring([ar.alloc((6, 256), BF16) for _ in range(2)], "avv")
        at_r = Ring([ar.alloc((8, 512), BF16) for _ in range(2)], "ato")
        pt_r = Ring([ar.alloc((512,), BF16) for _ in range(6)], "pt")
        sc_r = Ring([ar.alloc((512,), F32) for _ in range(3)], "sc")
        rd_r = Ring([ar.alloc((512,), F32) for _ in range(2)], "rd")
        zero_t = ar.alloc((512,), BF16); bzero = Buf("zero")
        S.op("pool", MSET(zero_t, 0.0), writes=[bzero])
        psr = psring()
        SCALE = 128.0 ** -0.5
        aqv = aqT_d.rearrange("(c p) t -> p c t", p=128)
        akv = akT_d.rearrange("(c p) t -> p c t", p=128)
        for ti in range(NT):
            t0 = ti * T
            qt, qtb = q_r.next(); kt, ktb = k_r.next(); vt, vtb = v_r.next(); at, atb = at_r.next()
            S.dma("sp", "sp_ld", DMA(qt, aqv[:, :, t0:t0 + T]), reads=[bscr["aqT"]], writes=[qtb])
            lo = max(t0 - 128, 0); hi = min(t0 + T + 128, NTOK)
            S.dma("sp", "sp_ld", DMA(kt[:, :, lo - (t0 - 128):hi - (t0 - 128)], akv[:, :, lo:hi]), reads=[bscr["akT"]], writes=[ktb])
            nb0 = (lo - (t0 - 128)) // 128; nb1 = (hi - (t0 - 128)) // 128
            S.dma("sp", "sp_ld", DMA(vt[:, nb0:nb1, :], avs_d[lo:hi, :].rearrange("(kb p) f -> p kb f", p=128)), reads=[bscr["avs"]], writes=[vtb])
            for bl in range(4):
                gb = ti * 4 + bl
                for g in range(2):
                    pts = []
                    for dk_ in (-1, 0, 1):
                        kb = gb + dk_
                        if kb < 0 or kb > 63:
                            continue
                        kl = bl + 1 + dk_
                        ps, psb = psr.next()
                        for hh in range(4):
                            S.op("pe", MM(ps[:, hh * 128:(hh + 1) * 128], kt[:, g, kl * 128:(kl + 1) * 128], qt[:, 4 * g + hh, bl * 128:(bl + 1) * 128]),
                                 reads=[ktb, qtb], writes=[psb])
                        pt, ptb = pt_r.next()
                        if dk_ == 0:
                            S.op("act", ACTV(pt, ps, AF.Exp, scale=SCALE), reads=[psb], writes=[ptb])
                        else:
                            sc, scb = sc_r.next()
                            S.op("dve", STT(sc, ps, SCALE, am_t[:, (0 if dk_ < 0 else 1), :], ALU.mult, ALU.add), reads=[psb, bam], writes=[scb])
                            S.op("act", ACTV(pt, sc, AF.Exp), reads=[scb], writes=[ptb])
                            if (kb // 32) != (gb // 32):
                                S.op("pool", TS(pt, pt, link[:, 0:1], None, ALU.mult), reads=[ptb, bconst], writes=[ptb])
                        pts.append((pt, ptb, kl))
                    po, pob = psr.next()
                    pd, pdb = psr.next()
                    for i_, (pt, ptb, kl) in enumerate(pts):
                        S.op("pe", MM(po, vt[:, kl, g * 128:(g + 1) * 128], pt, i_ == 0, i_ == len(pts) - 1), reads=[vtb, ptb], writes=[pob])
                    for i_, (pt, ptb, kl) in enumerate(pts):
                        S.op("pe", MM(pd, onesb[:, :], pt, i_ == 0, i_ == len(pts) - 1), reads=[bconst, ptb], writes=[pdb])
                    rd, rdb = rd_r.next()
                    for hh in range(4):
                        S.op("dve", TS(rd[:, hh * 128:(hh + 1) * 128], pd[:, hh * 128:(hh + 1) * 128], es_t[:, 4 * g + hh:4 * g + hh + 1], None, ALU.add),
                             reads=[pdb, bes], writes=[rdb])
                    S.op("dve", lambda e, rd=rd: e.reciprocal(out=rd, in_=rd), reads=[rdb], writes=[rdb])
                    S.op("dve", TT(at[:, 4 * g:4 * g + 4, bl * 128:(bl + 1) * 128], po.rearrange("p (h q) -> p h q", h=4),
                                   rd.rearrange("p (h q) -> p h q", h=4), ALU.mult), reads=[pob, rdb], writes=[atb])
            S.dma("sp", "sp_st", DMA(atT_d.rearrange("(c p) t -> p c t", p=128)[:, :, t0:t0 + T], at), reads=[atb], writes=[bscr["atT"]])
        S.barrier()
        if dev == 2:
            return finish()

        ar.reset()
        bk = Buf("dconst")
        convw = ar.alloc((24, 5), F32)
        dmk = ar.alloc((6, 128), F32)
        trit = ar.alloc((3, 128), F32)
        negb = ar.alloc((2, 128), BF16)
        hmk = ar.alloc((2,), F32)
        alog = ar.alloc((2, 4, 8), F32)
        dtb = ar.alloc((2, 4, 8), F32)
        negA = ar.alloc((2, 4, 8), F32)
        dnn = ar.alloc((128,), F32)
        onesf = ar.alloc((1,), F32)
        eps1 = ar.alloc((1,), F32)
        diagW = ar.alloc((24, 5, 128), BF16)
        for dst, src in ((convw, convw_d), (dmk, dmask_d), (trit, tri_d), (hmk, hmask_d), (dnn, dnn_d)):
            S.dma("sp", "sp_ld", DMA(dst, src), reads=[bin_], writes=[bk])
        S.dma("sp", "sp_ld", DMA(alog.rearrange("p a b c -> p (a b c)"), alog_d), reads=[bin_], writes=[bk])
        S.dma("sp", "sp_ld", DMA(dtb.rearrange("p a b c -> p (a b c)"), dtb_d), reads=[bin_], writes=[bk])
        S.op("act", ACTV(negA, alog, AF.Exp), reads=[bk], writes=[bk])
        S.op("dve", TS(negA, negA, -1.0, None, ALU.mult), reads=[bk], writes=[bk])
        S.op("dve", CP(negb, dmk[:, 0:2, :]), reads=[bk], writes=[bk])
        S.op("dve", MSET(onesf, 1.0), writes=[bk])
        S.op("dve", MSET(eps1, EPS), writes=[bk])
        bdw = Buf("diagW")
        for c in range(24):
            for j in range(5):
                S.op(("dve" if (c * 5 + j) % 2 else "pool"), TS(diagW[:, c, j, :], identf[:, :], convw[:, c, j:j + 1], None, ALU.mult),
                     reads=[bconst, bk], writes=[bdw])

        bankb = [Buf("bank%d" % i, excl=True) for i in range(8)]

        class PRing(Ring):
            def __init__(self, items, name):
                self.items = items
                self.i = 0

        def hq(h, k):
            return psum[h][:, k * 128:(k + 1) * 128], bankb[h]

        def hh2(h, k):
            return psum[h][:, k * 256:(k + 1) * 256], bankb[h]

        Pr = PRing([(psum[i][:, :], bankb[i]) for i in range(8)], "pp")
        F32r = PRing([(psum[0][:, q * 128:(q + 1) * 128], bankb[0]) for q in range(4)], "p32")
        Qr = PRing([(psum[1 + i][:, q * 128:(q + 1) * 128], bankb[1 + i]) for i in range(3) for q in range(4)], "pq")
        Hr = PRing([(psum[4 + i][:, hh * 256:(hh + 1) * 256], bankb[4 + i]) for i in range(2) for hh in range(2)], "ph")
        Fr = PRing([(psum[6 + i][:, :], bankb[6 + i]) for i in range(2)], "pf")

        def tl(shape, dt, name):
            return (ar.alloc(shape, dt), Buf(name))

        qkt, qkb = tl((24, 516), BF16, "qkt")
        bat, bab = tl((4, 32), F32, "bat")
        oacc, boa = tl((4, 8, 128), F32, "oacc")
        R3 = ar.alloc((4, 8, 128), F32); bR3 = Buf("R3")
        ob_t = R3
        R3h = R3.rearrange("p a b c -> p (a b c)").bitcast(BF16)
        dn_t = R3h[:, 0:4096].rearrange("p (a b) -> p a b", a=4)
        dst_t = R3h[:, 4096:8192].rearrange("p (a b) -> p a b", a=8)
        zs_t, bzs = tl((4, 1024), BF16, "zst")
        kt_all, bkt = tl((8, 128), F32, "kt")
        sqk, bsqk = tl((8, 128), F32, "sqk")
        sqo, bsqo = sqk, bsqk
        v_all, bva = tl((8, 128), BF16, "va")
        qT2 = [tl((8, 128), BF16, "qa%d" % i) for i in range(2)]
        qsq, bqs = tl((8, 128), BF16, "qsq")
        kn_all, bkn = tl((8, 128), BF16, "kn")
        knT_all, bknT = tl((8, 128), BF16, "knT")
        bet, bbet = tl((4, 8), F32, "bet")
        g_t_, bgg = tl((4, 8), F32, "g")
        ta_, bta = tl((4, 8), F32, "ta")
        gc_, bgc = tl((4, 8), F32, "gc")
        ngc, bngc = tl((4, 8), F32, "ngc")
        egc, begc = tl((4, 8), F32, "egc")
        kds, bkds = tl((4, 8), F32, "kds")
        glb, bglb = tl((4, 8, 2), F32, "glb")
        ghm, bghm = tl((4, 8, 2), F32, "ghm")
        nbh, bnbh = tl((4, 8, 2), F32, "nbh")
        bh_, bbh = tl((4, 8, 2), F32, "bh")
        ssk, bssk = tl((8,), F32, "ssk")
        cq, bcq = tl((8,), F32, "cq")
        cqh2 = [tl((8, 2), F32, "cqh%d" % i) for i in range(2)]
        ecq2 = [tl((8, 2), F32, "ecq%d" % i) for i in range(2)]
        sso, bsso = tl((32,), F32, "sso")
        def ph(n, shape, dt, name):
            return [tl(shape, dt, "%s%d" % (name, i)) for i in range(n)]
        dec = ph(8, (128,), F32, "dec"); LT = ph(8, (128,), F32, "LT")
        gtri = LT
        ones32 = ar.alloc((128,), F32)
        S.op("dve", MSET(ones32, 1.0), writes=[bk])
        LdT = ph(8, (128,), BF16, "LdT"); LoT = ph(8, (128,), BF16, "LoT"); Ld = ph(8, (128,), BF16, "Ld")
        PRa = ph(8, (256,), BF16, "PRa"); PRb = ph(8, (256,), BF16, "PRb")
        Pa = ph(8, (128,), BF16, "Pa"); Pb = ph(8, (128,), BF16, "Pb")
        XT = ph(8, (128,), BF16, "XT"); ke = ph(8, (128,), BF16, "ke")
        yy = ph(8, (256,), BF16, "yy"); rr_ = ph(8, (256,), BF16, "rr")
        ub = ph(8, (128,), BF16, "ub"); t2 = dec
        u0b = ph(8, (2, 128), F32, "u0b"); wT = ph(8, (128,), BF16, "wT")
        qkT = ph(8, (128,), BF16, "qkT"); kdec = ph(8, (128,), BF16, "kdec")
        Sst = ph(8, (128,), F32, "S"); Sb_ = ph(8, (128,), BF16, "Sb")
        qkvv = qkvT_d.rearrange("(c p) t -> p c t", p=128)
        X = mybir.AxisListType.X
        print("phase2b arena use (bf16 elems)", ar.off, "of", ar.n)

        def bf(ps):
            return ps.bitcast(BF16)

        class _Stop(Exception):
            pass

        def ck(n):
            if dev == n:
                raise _Stop()

        try:
          ck(22)
          for d in (1, 0):
            for h in range(8):
                S.op("pool", MSET(Sst[h][0], 0.0), writes=[Sst[h][1]])
                S.op("pool", MSET(Sb_[h][0], 0.0), writes=[Sb_[h][1]])
            tiles = list(range(NT)) if d == 0 else list(range(NT - 1, -1, -1))
            if dev == 21:
                tiles = tiles[:2]
            for ti in tiles:
                t0 = ti * T
                if (d == 0 and ti == NT // 2) or (d == 1 and ti == NT // 2 - 1):
                    for h in range(8):
                        S.op("dve", TS(Sst[h][0], Sst[h][0], link[:, 0:1], None, ALU.mult), reads=[Sst[h][1], bconst], writes=[Sst[h][1]])
                        S.op("act", ACTV(Sb_[h][0], Sst[h][0], AF.Copy), reads=[Sst[h][1]], writes=[Sb_[h][1]])
                lo = max(t0 - 2, 0); hi = min(t0 + T + 2, NTOK)
                for c8 in range(3):
                    S.dma("sp", "sp_ld", DMA(qkt[:, c8 * 8:(c8 + 1) * 8, lo - (t0 - 2):hi - (t0 - 2)], qkvv[:, c8 * 8:(c8 + 1) * 8, lo:hi]),
                          reads=[bscr["qkvT"]], writes=[qkb])
                if ti == 0:
                    S.op("pool", MSET(qkt[:, :, 0:2], 0.0), writes=[qkb])
                if ti == NT - 1:
                    S.op("pool", MSET(qkt[:, :, 514:516], 0.0), writes=[qkb])
                if ti == NT // 2 - 1:
                    S.op("pool", TS(qkt[:, :, 514:516], qkt[:, :, 514:516], link[:, 0:1], None, ALU.mult), reads=[qkb, bconst], writes=[qkb])
                if ti == NT // 2:
                    S.op("pool", TS(qkt[:, :, 0:2], qkt[:, :, 0:2], link[:, 0:1], None, ALU.mult), reads=[qkb, bconst], writes=[qkb])
                S.dma("sp", "sp_ld", DMA(bat, ba_d[t0:t0 + T, :].rearrange("(bl p) f -> p bl f", p=128)), reads=[bscr["ba"]], writes=[bab])
                ck(231)
                S.op("act", ACTV(bet, bat[:, :, d * 8:(d + 1) * 8], AF.Sigmoid), reads=[bab], writes=[bbet])
                S.op("dve", TT(ta_, bat[:, :, 16 + d * 8:16 + (d + 1) * 8], dtb[:, d, :, :], ALU.add), reads=[bab, bk], writes=[bta])
                S.op("act", ACTV(ta_, ta_, AF.Exp), reads=[bta], writes=[bta])
                S.op("act", ACTV(ta_, ta_, AF.Ln, bias=onesf[:, 0:1], scale=1.0), reads=[bta, bk], writes=[bta])
                S.op("dve", TT(g_t_, ta_, negA[:, d, :, :], ALU.mult), reads=[bta, bk], writes=[bgg])
                ck(232)
                _pb, _pbb = Pr.next(); pg1, pg1b = _pb[:, 0:128], _pbb; pg2, pg2b = _pb[:, 128:256], _pbb; pg3, pg3b = _pb[:, 256:384], _pbb
                for bl in range(4):
                    S.op("pe", MM(pg1[:, bl * 8:(bl + 1) * 8], trit[:, d, :], g_t_[:, bl, :]), reads=[bk, bgg], writes=[pg1b])
                    S.op("pe", MM(pg2[:, bl * 8:(bl + 1) * 8], trit[:, 2, :], g_t_[:, bl, :]), reads=[bk, bgg], writes=[pg2b])

                for hf in range(2):
                    S.op("dve", TS(ghm[:, :, :, hf], g_t_, hmk[:, hf:hf + 1], None, ALU.mult), reads=[bgg, bk], writes=[bghm])
                S.op("pe", MM(pg3[:, 0:64], ones32, ghm.rearrange("p a b c -> p (a b c)")), reads=[bghm, bk], writes=[pg3b])
                ck(233)
                fl = lambda a: a.rearrange("p a b -> p (a b)")
                if dev != 2343:
                    S.op("dve", CP(fl(gc_), pg1[:, 0:32]), reads=[pg1b], writes=[bgc])
                ck(2341)
                if dev != 2343:
                    S.op("dve", TS(fl(ngc), pg1[:, 0:32], -1.0, None, ALU.mult), reads=[pg1b], writes=[bngc])
                if dev == 2342:
                    S.dma("sp", "sp_st", DMA(gdbg[:, 0:32], fl(g_t_)), reads=[bgg], writes=[bscr["dbg"]])
                    S.dma("sp", "sp_st", DMA(gdbg[:, 32:64], fl(gc_)), reads=[bgc], writes=[bscr["dbg"]])
                    S.dma("sp", "sp_st", DMA(gdbg[:, 64:96], fl(ngc)), reads=[bngc], writes=[bscr["dbg"]])
                    S.dma("sp", "sp_st", DMA(gdbg[:, 96:128], fl(bet)), reads=[bbet], writes=[bscr["dbg"]])
                ck(2342)
                S.op("act", ACTV(fl(egc), fl(gc_), AF.Exp), reads=[bgc], writes=[begc])
                ck(2343)
                ck(234)
                S.op("dve", TT(fl(kds), pg2[:, 0:32], fl(gc_), ALU.subtract), reads=[pg2b, bgc], writes=[bkds])
                S.op("act", ACTV(fl(kds), fl(kds), AF.Exp), reads=[bkds], writes=[bkds])
                ck(235)
                S.op("dve", CP(glb.rearrange("p a b c -> p (a b c)"), pg3[:, 0:64]), reads=[pg3b], writes=[bglb])
                S.op("act", ACTV(glb.rearrange("p a b c -> p (a b c)"), glb.rearrange("p a b c -> p (a b c)"), AF.Exp), reads=[bglb], writes=[bglb])
                ck(236)
                for hf in range(2):
                    S.op("dve", TS(nbh[:, :, :, hf], bet, hmk[:, hf:hf + 1], -1.0, ALU.mult, ALU.mult), reads=[bbet, bk], writes=[bnbh])
                    S.op("dve", TS(bh_[:, :, :, hf], bet, hmk[:, hf:hf + 1], None, ALU.mult), reads=[bbet, bk], writes=[bbh])

                ck(23)
                def prep_chunks(bl, par):
                    c0 = bl * 128
                    qT_all, bqa = qT2[par]
                    cqh, bcqh = cqh2[par]
                    ecq, becq = ecq2[par]
                    out = []
                    for (cb, dstt, dstb, tokmaj) in ((8, kt_all, bkt, True), (16, v_all, bva, True), (0, qT_all, bqa, False)):
                        for hg in range(2):
                            def conv(cb=cb, dstt=dstt, dstb=dstb, tokmaj=tokmaj, hg=hg):
                                ps, psb = Pr.next()
                                for hh in range(4):
                                    h = hg * 4 + hh
                                    for j in range(5):
                                        if tokmaj:
                                            S.op("pe", MM(ps[:, hh * 128:(hh + 1) * 128], qkt[:, cb + h, c0 + j:c0 + j + 128], diagW[:, cb + h, j, :], j == 0, j == 4),
                                                 reads=[qkb, bdw], writes=[psb])
                                        else:
                                            S.op("pe", MM(ps[:, hh * 128:(hh + 1) * 128], diagW[:, cb + h, j, :], qkt[:, cb + h, c0 + j:c0 + j + 128], j == 0, j == 4),
                                                 reads=[qkb, bdw], writes=[psb])
                                S.op("act", ACTV(dstt[:, hg * 4:(hg + 1) * 4, :].rearrange("p a b -> p (a b)"), ps, AF.Silu), reads=[psb], writes=[dstb])
                            out.append(conv)

                    def knorm():
                        S.op("act", ACTV(sqk, kt_all, AF.Square), reads=[bkt], writes=[bsqk])
                        S.op("dve", lambda e: e.tensor_reduce(out=ssk, in_=sqk, axis=X, op=ALU.add), reads=[bsqk], writes=[bssk])
                        S.op("act", ACTV(ssk, ssk, AF.Sqrt, bias=eps1[:, 0:1], scale=1.0), reads=[bssk, bk], writes=[bssk])
                        S.op("dve", lambda e: e.reciprocal(out=ssk, in_=ssk), reads=[bssk], writes=[bssk])
                        for h in range(8):
                            S.op("dve", TS(kn_all[:, h, :], kt_all[:, h, :], ssk[:, h:h + 1], None, ALU.mult), reads=[bkt, bssk], writes=[bkn])

                    def ktr():
                        ps, psb = Pr.next()
                        for h in range(8):
                            S.op("pe", TR(bf(ps)[:, h * 128:(h + 1) * 128], kn_all[:, h, :], identb[:, :]), reads=[bkn, bconst], writes=[psb])
                        S.op("act", ACTV(knT_all.rearrange("p a b -> p (a b)"), bf(ps)[:, 0:1024], AF.Copy), reads=[psb], writes=[bknT])

                    def qnorm():
                        S.op("act", ACTV(qsq, qT_all, AF.Square), reads=[bqa], writes=[bqs])
                        ps, psb = Pr.next()
                        for h in range(8):
                            S.op("pe", MM(ps[:, h:h + 1], qsq[:, h, :], onesb[:, 0:1]), reads=[bqs, bconst], writes=[psb])
                        S.op("act", ACTV(cq, ps[:, 0:8], AF.Sqrt, bias=eps1[:, 0:1], scale=1.0), reads=[psb, bk], writes=[bcq])
                        S.op("dve", lambda e: e.reciprocal(out=cq, in_=cq), reads=[bcq], writes=[bcq])
                        for hf in range(2):
                            S.op("dve", TS(cqh[:, :, hf], cq, hmk[:, hf:hf + 1], 128.0 ** -0.5, ALU.mult, ALU.mult), reads=[bcq, bk], writes=[bcqh])
                            S.op("dve", TT(ecq[:, :, hf], cqh[:, :, hf], egc[:, bl, :], ALU.mult), reads=[bcqh, begc], writes=[becq])
                    out += [knorm, ktr, qnorm]
                    return out

                order = list(range(4)) if d == 0 else [3, 2, 1, 0]
                for c_ in prep_chunks(order[0], 0):
                    c_()
                for kk_, bl in enumerate(order):
                    qT_all, bqa = qT2[kk_ % 2]
                    cqh, bcqh = cqh2[kk_ % 2]
                    ecq, becq = ecq2[kk_ % 2]
                    nxt = prep_chunks(order[kk_ + 1], (kk_ + 1) % 2) if kk_ + 1 < 4 else []
                    ck(24)
                    for hg in range(1):
                        HS = list(range(8))
                        cur = {}

                        def st_a(h):
                            i = h
                            pe_, peb = hq(h, 0); pg, pgb = hq(h, 1); pq, pqb = hq(h, 2)
                            S.op("act", ACTV(gtri[i][0], trit[:, d, :], AF.Copy, scale=g_t_[:, bl, h:h + 1]), reads=[bgg, bk], writes=[gtri[i][1]])
                            S.op("pe", MM(pe_, ones32, gtri[i][0]), reads=[gtri[i][1], bk], writes=[peb])
                            S.op("pe", MM(pg, knT_all[:, h, :], knT_all[:, h, :]), reads=[bknT], writes=[pgb])
                            S.op("pe", MM(pq, knT_all[:, h, :], qT_all[:, h, :]), reads=[bknT, bqa], writes=[pqb])
                            cur[h] = dict(pe=(pe_, peb), pg=(pg, pgb), pq=(pq, pqb))

                        def st_b1(h):
                            i = h; c = cur[h]
                            S.op("dve", STT(dec[i][0], c["pe"][0], ngc[:, bl, h:h + 1], dmk[:, d, :], ALU.add, ALU.add), reads=[c["pe"][1], bk, bngc], writes=[dec[i][1]])

                        def st_b2(h):
                            i = h
                            S.op("act", ACTV(dec[i][0], dec[i][0], AF.Exp), reads=[dec[i][1]], writes=[dec[i][1]])

                        def st_b3(h):
                            i = h; c = cur[h]
                            S.op("dve", STT(LT[i][0], c["pg"][0], bet[:, bl, h:h + 1], dec[i][0], ALU.mult, ALU.mult), reads=[c["pg"][1], bbet, dec[i][1]], writes=[LT[i][1]])
                            S.op("dve", TT(qkT[h][0], c["pq"][0], dec[i][0], ALU.mult), reads=[c["pq"][1], dec[i][1]], writes=[qkT[h][1]])

                        def st_b4(h):
                            i = h
                            S.op("dve", TT(LdT[i][0], LT[i][0], dmk[:, 2 + d, :], ALU.mult), reads=[LT[i][1], bk], writes=[LdT[i][1]])
                            S.op("dve", TT(LoT[i][0], LT[i][0], dmk[:, 4 + d, :], ALU.mult), reads=[LT[i][1], bk], writes=[LoT[i][1]])
                            S.op("dve", TT(PRa[i][0][:, 128:256], identb[:, :], LdT[i][0], ALU.subtract), reads=[bconst, LdT[i][1]], writes=[PRa[i][1]])

                        def st_c(h):
                            i = h
                            pt, ptb = hq(h, 3)
                            S.op("pe", TR(bf(pt)[:, 0:128], LdT[i][0], identb[:, :]), reads=[LdT[i][1], bconst], writes=[ptb])
                            cur[h]["pt"] = (pt, ptb)

                        def st_d(h):
                            i = h
                            S.op("act", ACTV(Ld[i][0], bf(cur[h]["pt"][0])[:, 0:128], AF.Copy), reads=[cur[h]["pt"][1]], writes=[Ld[i][1]])

                        def st_e(h):
                            i = h
                            p1t, p1tb = hq(h, 0); p1, p1b = hq(h, 1)
                            if dev == 2553:
                                S.op("pe", MM(p1t, LdT[i][0], LdT[i][0]), reads=[Ld[i][1], LdT[i][1]], writes=[p1tb])
                            elif dev != 2552:
                                S.op("pe", MM(p1t, Ld[i][0], LdT[i][0]), reads=[Ld[i][1], LdT[i][1]], writes=[p1tb])
                            if dev not in (2551, 2553):
                                S.op("pe", MM(p1, LdT[i][0], Ld[i][0]), reads=[Ld[i][1], LdT[i][1]], writes=[p1b])
                            cur[h]["a"] = (p1t, p1tb); cur[h]["b"] = (p1, p1b)

                        def st_f(h):
                            i = h
                            S.op("act", ACTV(PRa[i][0][:, 0:128], cur[h]["a"][0], AF.Copy), reads=[cur[h]["a"][1]], writes=[PRa[i][1]])
                            S.op("act", ACTV(Pa[i][0], cur[h]["b"][0], AF.Copy), reads=[cur[h]["b"][1]], writes=[Pa[i][1]])

                        def mk_lvl(PRs, Ps, PRd, Pd, hsel, qsel):
                            def mm(h):
                                i = h
                                pA, pAb = hh2(h, hsel); pB, pBb = hq(h, qsel)
                                S.op("pe", MM(pA, Ps[i][0], PRs[i][0]), reads=[Ps[i][1], PRs[i][1]], writes=[pAb])
                                S.op("pe", MM(pB, PRs[i][0][:, 0:128], Ps[i][0]), reads=[Ps[i][1], PRs[i][1]], writes=[pBb])
                                cur[h]["a"] = (pA, pAb); cur[h]["b"] = (pB, pBb)

                            def ev(h):
                                i = h
                                pA, pAb = cur[h]["a"]; pB, pBb = cur[h]["b"]
                                S.op("act", ACTV(PRd[i][0][:, 0:128], pA[:, 0:128], AF.Copy), reads=[pAb], writes=[PRd[i][1]])
                                S.op("act", ACTV(Pd[i][0], pB, AF.Copy), reads=[pBb], writes=[Pd[i][1]])
                                S.op("dve", TT(PRd[i][0][:, 128:256], PRs[i][0][:, 128:256], pA[:, 128:256], ALU.add), reads=[pAb, PRs[i][1]], writes=[PRd[i][1]])
                            return mm, ev

                        l1m, l1e = mk_lvl(PRa, Pa, PRb, Pb, 1, 0)
                        l2m, l2e = mk_lvl(PRb, Pb, PRa, Pa, 0, 2)

                        def st_g(h):
                            i = h
                            pA, pAb = hq(h, 0); pB, pBb = hq(h, 1)
                            S.op("pe", MM(pA, Pa[i][0], PRa[i][0][:, 128:256]), reads=[Pa[i][1], PRa[i][1]], writes=[pAb])
                            S.op("pe", MM(pB, PRa[i][0][:, 0:128], Pa[i][0]), reads=[Pa[i][1], PRa[i][1]], writes=[pBb])
                            cur[h]["a"] = (pA, pAb); cur[h]["b"] = (pB, pBb)

                        def st_h(h):
                            i = h
                            S.op("act", ACTV(Pb[i][0], cur[h]["b"][0], AF.Copy), reads=[cur[h]["b"][1]], writes=[Pb[i][1]])
                            S.op("dve", TT(PRb[i][0][:, 128:256], PRa[i][0][:, 128:256], cur[h]["a"][0], ALU.add), reads=[cur[h]["a"][1], PRa[i][1]], writes=[PRb[i][1]])

                        def st_i(h):
                            i = h
                            px, pxb = hq(h, 2)
                            S.op("pe", MM(px, Pb[i][0], PRb[i][0][:, 128:256]), reads=[Pb[i][1], PRb[i][1]], writes=[pxb])
                            cur[h]["a"] = (px, pxb)

                        def st_j(h):
                            i = h
                            S.op("dve", TT(XT[i][0], PRb[i][0][:, 128:256], cur[h]["a"][0], ALU.add), reads=[cur[h]["a"][1], PRb[i][1]], writes=[XT[i][1]])
                            S.op("act", ACTV(ke[i][0], kn_all[:, h, :], AF.Copy, scale=egc[:, bl, h:h + 1]), reads=[bkn, begc], writes=[ke[i][1]])
                            S.op("act", ACTV(kdec[h][0], kn_all[:, h, :], AF.Copy, scale=kds[:, bl, h:h + 1]), reads=[bkn, bkds], writes=[kdec[h][1]])

                        def st_k(h):
                            i = h
                            py, pyb = hh2(h, 1)
                            S.op("pe", MM(py[:, 0:128], XT[i][0], v_all[:, h, :]), reads=[XT[i][1], bva], writes=[pyb])
                            S.op("pe", MM(py[:, 128:256], XT[i][0], ke[i][0]), reads=[XT[i][1], ke[i][1]], writes=[pyb])
                            cur[h]["a"] = (py, pyb)

                        def st_l(h):
                            i = h
                            S.op("act", ACTV(yy[i][0], cur[h]["a"][0], AF.Copy), reads=[cur[h]["a"][1]], writes=[yy[i][1]])

                        def st_m(h):
                            i = h
                            pz, pzb = hh2(h, 0)
                            S.op("pe", MM(pz, LoT[i][0], yy[i][0]), reads=[LoT[i][1], yy[i][1]], writes=[pzb])
                            cur[h]["a"] = (pz, pzb)

                        def st_n(h):
                            i = h
                            pz, pzb = cur[h]["a"]
                            S.op("dve", TT(rr_[i][0][:, 0:128], v_all[:, h, :], pz[:, 0:128], ALU.subtract), reads=[pzb, bva], writes=[rr_[i][1]])
                            S.op("dve", TT(rr_[i][0][:, 128:256], ke[i][0], pz[:, 128:256], ALU.subtract), reads=[pzb, ke[i][1]], writes=[rr_[i][1]])

                        def st_o(h):
                            i = h
                            pu, pub = hq(h, 2); pw, pwb = hq(h, 3)
                            S.op("pe", MM(pu, XT[i][0], rr_[i][0][:, 0:128]), reads=[XT[i][1], rr_[i][1]], writes=[pub])
                            S.op("pe", MM(pw, rr_[i][0][:, 128:256], XT[i][0]), reads=[XT[i][1], rr_[i][1]], writes=[pwb])
                            cur[h]["a"] = (pu, pub); cur[h]["b"] = (pw, pwb)

                        def st_p(h):
                            pu, pub = cur[h]["a"]; pw, pwb = cur[h]["b"]
                            S.op("act", ACTV(u0b[h][0][:, 0, :], pu, AF.Copy, scale=bh_[:, bl, h, 0:1]), reads=[pub, bbh], writes=[u0b[h][1]])
                            S.op("act", ACTV(u0b[h][0][:, 1, :], pu, AF.Copy, scale=bh_[:, bl, h, 1:2]), reads=[pub, bbh], writes=[u0b[h][1]])
                            S.op("act", ACTV(wT[h][0], pw, AF.Copy), reads=[pwb], writes=[wT[h][1]])

                        safe_pts = {}
                        stages = [st_a, st_b1, st_b2, st_b3, st_b4, st_c, st_d, st_e, st_f, l1m, l1e, l2m, l2e, st_g, st_h, st_i, st_j, st_k, st_l, st_m, st_n, st_o, st_p]
                        for k_, hf in enumerate((0, 1) if d == 0 else (1, 0)):
                            def sc_a(h):
                                pws, pwsb = hq(h, 0)
                                S.op("pe", MM(pws, wT[h][0], Sb_[h][0]), reads=[wT[h][1], Sb_[h][1]], writes=[pwsb])
                                cur[h]["a"] = (pws, pwsb)

                            def sc_b(h, hf=hf):
                                i = h
                                S.op("dve", STT(ub[i][0], cur[h]["a"][0], nbh[:, bl, h, hf:hf + 1], u0b[h][0][:, hf, :], ALU.mult, ALU.add),
                                     reads=[cur[h]["a"][1], bnbh, u0b[h][1]], writes=[ub[i][1]])

                            def sc_c(h):
                                i = h
                                p1_, p1b_ = hq(h, 1); p2_, p2b_ = hq(h, 2); pS, pSb = hq(h, 3)
                                S.op("pe", MM(p1_, qkT[h][0], ub[i][0]), reads=[qkT[h][1], ub[i][1]], writes=[p1b_])
                                S.op("pe", MM(p2_, qT_all[:, h, :], Sb_[h][0]), reads=[bqa, Sb_[h][1]], writes=[p2b_])
                                S.op("pe", MM(pS, kdec[h][0], ub[i][0]), reads=[kdec[h][1], ub[i][1]], writes=[pSb])
                                cur[h]["o1"] = (p1_, p1b_); cur[h]["o2"] = (p2_, p2b_); cur[h]["s"] = (pS, pSb)

                            def sc_d1(h, hf=hf):
                                i = h
                                S.op("act", ACTV(t2[i][0], cur[h]["o2"][0], AF.Copy, scale=ecq[:, h, hf:hf + 1]), reads=[cur[h]["o2"][1], becq], writes=[t2[i][1]])

                            def sc_d2(h, hf=hf, k_=k_):
                                i = h
                                S.op("dve", STT(Sst[h][0], Sst[h][0], glb[:, bl, h, hf:hf + 1], cur[h]["s"][0], ALU.mult, ALU.add),
                                     reads=[Sst[h][1], bglb, cur[h]["s"][1]], writes=[Sst[h][1]])
                                if k_ == 0:
                                    S.op("dve", STT(oacc[:, bl, h, :], cur[h]["o1"][0], cqh[:, h, hf:hf + 1], t2[i][0], ALU.mult, ALU.add),
                                         reads=[cur[h]["o1"][1], bcqh, t2[i][1]], writes=[boa])
                                else:
                                    S.op("dve", STT(t2[i][0], cur[h]["o1"][0], cqh[:, h, hf:hf + 1], t2[i][0], ALU.mult, ALU.add),
                                         reads=[cur[h]["o1"][1], bcqh, t2[i][1]], writes=[t2[i][1]])

                            def sc_d3(h, k_=k_):
                                i = h
                                S.op("act", ACTV(Sb_[h][0], Sst[h][0], AF.Copy), reads=[Sst[h][1]], writes=[Sb_[h][1]])
                                if k_ == 1:
                                    S.op("dve", TT(oacc[:, bl, h, :], oacc[:, bl, h, :], t2[i][0], ALU.add), reads=[boa, t2[i][1]], writes=[boa])
                            stages += [sc_a, sc_b, sc_c, sc_d1, sc_d2, sc_d3]
                            safe_pts[sc_b] = 2
                            safe_pts[sc_d3] = 3
                        if 250 <= dev < 280:
                            stages = stages[:dev - 250]
                        if dev in (2521, 2522, 2523, 2524, 2525, 2526):
                            stages = stages[:2]
                        if dev in (2551, 2552, 2553):
                            stages = stages[:5]
                        if dev == 25:
                            stages = stages[:6]
                        if dev == 26:
                            stages = stages[:14]
                        if dev == 27:
                            stages = stages[:20]
                        for f in stages:
                            for h in HS:
                                f(h)
                            if f in safe_pts:
                                for _ in range(safe_pts[f]):
                                    if nxt:
                                        nxt.pop(0)()
                        while nxt:
                            nxt.pop(0)()
                        if dev == 254 and hg == 0:
                            for i in range(4):
                                S.dma("sp", "sp_st", DMA(hdbg[:, i * 256:i * 256 + 128], LdT[i][0]), reads=[LdT[i][1]], writes=[bscr["dbg"]])
                                S.dma("sp", "sp_st", DMA(hdbg[:, i * 256 + 128:i * 256 + 256], Ld[i][0]), reads=[Ld[i][1]], writes=[bscr["dbg"]])
                            S.dma("sp", "sp_st", DMA(gdbg[:, :], dec[0][0]), reads=[dec[0][1]], writes=[bscr["dbg"]])
                        if dev in (25, 26, 27, 2521, 2522, 2523, 2524, 2525, 2526, 2551, 2552, 2553) or 250 <= dev < 280:
                            raise _Stop()
                if d == 1:
                    S.dma("sp", "sp_st", DMA(obs_d[t0:t0 + T, :].rearrange("(bl p) f -> p bl f", p=128), oacc.rearrange("p a b c -> p a (b c)")),
                          reads=[boa], writes=[bscr["obs"]])
                else:
                    S.dma("sp", "sp_ld", DMA(ob_t.rearrange("p a b c -> p a (b c)"), obs_d[t0:t0 + T, :].rearrange("(bl p) f -> p bl f", p=128)),
                          reads=[bscr["obs"]], writes=[bR3])
                    S.dma("sp", "sp_ld", DMA(zs_t, zs_d[t0:t0 + T, :].rearrange("(bl p) f -> p bl f", p=128)), reads=[bscr["zs"]], writes=[bzs])
                    if dev:
                        S.dma("sp", "sp_st", DMA(ofs_d[t0:t0 + T, :].rearrange("(bl p) f -> p bl f", p=128), oacc.rearrange("p a b c -> p a (b c)")),
                              reads=[boa], writes=[bscr["dbg"]])
                    oflat = oacc.rearrange("p a b c -> p (a b c)")
                    S.op("dve", TT(oflat, oflat, ob_t.rearrange("p a b c -> p (a b c)"), ALU.add), reads=[boa, bR3], writes=[boa])
                    for bl in range(4):
                        S.op("pool", TT(sqo, oacc[:, bl, :, :], oacc[:, bl, :, :], ALU.mult), reads=[boa], writes=[bsqo])
                        S.op("dve", lambda e, bl=bl: e.tensor_reduce(out=sso[:, bl * 8:(bl + 1) * 8], in_=sqo, axis=X, op=ALU.add), reads=[bsqo], writes=[bsso])
                    S.op("act", ACTV(sso, sso, AF.Sqrt, bias=eps1[:, 0:1], scale=1.0 / 128.0), reads=[bsso, bk], writes=[bsso])
                    S.op("dve", lambda e: e.reciprocal(out=sso, in_=sso), reads=[bsso], writes=[bsso])
                    for bl in range(4):
                        for h in range(8):
                            S.op("dve", STT(oacc[:, bl, h, :], oacc[:, bl, h, :], sso[:, bl * 8 + h:bl * 8 + h + 1], dnn, ALU.mult, ALU.mult),
                                 reads=[boa, bsso, bk], writes=[boa])
                    S.op("dve", TT(dn_t.rearrange("p a b -> p (a b)"), oflat, zs_t.rearrange("p a b -> p (a b)"), ALU.mult), reads=[boa, bzs, bR3], writes=[bR3])
                    for bl in range(4):
                        ps, psb = Pr.next()
                        for h in range(8):
                            S.op("pe", TR(bf(ps)[:, h * 128:(h + 1) * 128], dn_t[:, bl, h * 128:(h + 1) * 128], identb[:, :]), reads=[bR3, bconst], writes=[psb])
                        S.op("act", ACTV(dst_t[:, :, bl * 128:(bl + 1) * 128], bf(ps)[:, 0:1024].rearrange("p (a b) -> p a b", a=8), AF.Copy), reads=[psb, bR3], writes=[bR3])
                    S.dma("sp", "sp_st", DMA(dnT_d.rearrange("(c p) t -> p c t", p=128)[:, :, t0:t0 + T], dst_t), reads=[bR3], writes=[bscr["dnT"]])
            S.barrier()
        except _Stop:
            S.barrier()
            return finish()
        if dev in (3, 21):
            return finish()

        ar.reset()
        xT = ar.alloc((8, 512), F32); xTb = Buf("xT3")
        h = ar.alloc((8, 512), BF16); hb = Buf("h3")
        R1 = ar.alloc((24, 512), BF16); bR1 = Buf("R13")
        hid = R1
        dn_t = ar.alloc((8, 512), BF16); bdn = Buf("dnt")
        at_t = ar.alloc((8, 512), BF16); bat = Buf("att")
        g_t = ar.alloc((16, 512), BF16); bg = Buf("gt")
        yout = g_t.rearrange("p a b -> p (a b)").bitcast(F32).rearrange("p (a b) -> p a b", a=4)
        tmpA = ar.alloc((8, 512), F32); btA = Buf("tmpA")
        yT = tmpA
        mrg = ar.alloc((8, 512), BF16); bmr = Buf("mrg")
        tb_r = Ring([ar.alloc((512,), F32) for _ in range(2)], "tmpB")
        sqr = Ring([ar.alloc((512,), BF16) for _ in range(2)], "sq3")
        rsr = Ring([ar.alloc((512,), F32) for _ in range(2)], "rs3")
        tmr = Ring([ar.alloc((512,), F32) for _ in range(3)], "tm3")
        sgr = Ring([ar.alloc((512,), F32) for _ in range(3)], "sg3")
        wr = Ring([ar.alloc((22 * 256,), BF16) for _ in range(3)], "w3")
        psr = psring()
        rings = (sqr, rsr, tmr, psr)
        for ti in range(NT):
            t0 = ti * T
            s = ti // (NT // 2)
            S.dma("sp", "sp_ld", DMA(xT, x1T_d.rearrange("(kc p) t -> p kc t", p=128)[:, :, t0:t0 + T]), reads=[bscr["x1T"]], writes=[xTb])
            S.dma("sp", "sp_ld", DMA(dn_t, dnT_d.rearrange("(c p) t -> p c t", p=128)[:, :, t0:t0 + T]), reads=[bscr["dnT"]], writes=[bdn])
            S.dma("sp", "sp_ld", DMA(at_t, atT_d.rearrange("(c p) t -> p c t", p=128)[:, :, t0:t0 + T]), reads=[bscr["atT"]], writes=[bat])
            for c8 in range(2):
                S.dma("sp", "sp_ld", DMA(g_t[:, c8 * 8:(c8 + 1) * 8, :], gT_d.rearrange("(c p) t -> p c t", p=128)[:, c8 * 8:(c8 + 1) * 8, t0:t0 + T]),
                      reads=[bscr["gT"]], writes=[bg])

            def epi_a(g, outs):
                for q, (ps, psb) in enumerate(outs):
                    fo = g * 4 + q
                    S.op("dve", TT(tmpA[:, fo, :], ps, g_t[:, fo, :], ALU.mult), reads=[psb, bg], writes=[btA])
            gemm_fm(wb["w_proj_a"], 8, [[(g * 512, 512)] for g in range(2)], lambda kc: dn_t[:, kc, :], [bdn], epi_a, wr, psr)

            def epi_b(g, outs):
                for q, (ps, psb) in enumerate(outs):
                    fo = g * 4 + q
                    tb_, tbb = tb_r.next()
                    S.op("dve", TT(tb_, ps, g_t[:, 8 + fo, :], ALU.mult), reads=[psb, bg], writes=[tbb])
                    S.op("dve", TT(mrg[:, fo, :], tmpA[:, fo, :], tb_, ALU.add), reads=[btA, tbb], writes=[bmr])
            gemm_fm(wb["w_proj_b"], 8, [[(g * 512, 512)] for g in range(2)], lambda kc: at_t[:, kc, :], [bat], epi_b, wr, psr)

            def epi_o(g, outs, s=s):
                for q, (ps, psb) in enumerate(outs):
                    fo = g * 4 + q
                    S.op("dve", STT(xT[:, fo, :], ps, G_t[:, 1, s, fo:fo + 1], xT[:, fo, :], ALU.mult, ALU.add), reads=[psb, bmod, xTb], writes=[xTb])
            gemm_fm(wb["w_out"], 8, [[(g * 512, 512)] for g in range(2)], lambda kc: mrg[:, kc, :], [bmr], epi_o, wr, psr)
            rms_ada(xT, xTb, 2, s, h, hb, rings)
            swiglu_ffn("ffn2_w_in", "ffn2_w_out", h, hb, hid, bR1, xT, xTb, 2, s, wr, psr, sgr)
            rms_ada(xT, xTb, 3, s, yT, btA, rings)
            for tb in range(4):
                for k2 in range(2):
                    ps, psb = psr.next()
                    for kq in range(4):
                        kc = k2 * 4 + kq
                        S.op("pe", TR(ps[:, kq * 128:(kq + 1) * 128], yT[:, kc, tb * 128:(tb + 1) * 128], identf[:, :]), reads=[btA, bconst], writes=[psb])
                    if k2:
                        S.op("act", ACTV(yout[:, tb, k2 * 512:(k2 + 1) * 512], ps, AF.Copy), reads=[psb], writes=[bg])
                    else:
                        S.op("dve", CP(yout[:, tb, k2 * 512:(k2 + 1) * 512], ps), reads=[psb], writes=[bg])
            S.dma("sp", "sp_st", DMA(y_d[t0:t0 + T, :].rearrange("(tb p) f -> p tb f", p=128), yout), reads=[bg], writes=[bscr["y"]])
        S.barrier()
        st = S.emit(nc, sems, lambda n: es.enter_context(nc.semaphore(n)))
        print("instr counts", st)
    return nc


def host_inputs(inputs):
    f32 = np.float32
    g = {k: np.asarray(v) for k, v in inputs.items()}
    xp, xs = g["x_prompt"], g["x_sample"]
    cp, cs = g["c_prompt"], g["c_sample"]
    shared = {}
    shared["w_ada"] = np.ascontiguousarray(g["w_ada"][0])
    shared["b_adaT"] = np.ascontiguousarray(g["b_ada"][0].reshape(72, 128).T)
    nrm = np.stack([g["ffn1_norm"][0], g["mix_norm"][0], g["ffn2_norm"][0], g["final_norm"]], 0)
    shared["nrmT"] = np.ascontiguousarray(nrm.reshape(4, 8, 128).transpose(2, 0, 1))
    shared["conv_wT"] = np.ascontiguousarray(g["conv_w"][0].reshape(5, 24, 128).transpose(2, 1, 0))
    rep4 = lambda a: np.ascontiguousarray(np.broadcast_to(a.reshape(1, 2, 1, 8), (128, 2, 4, 8)).reshape(128, 64))
    shared["a_log_bc"] = rep4(g["a_log"][0])
    shared["dt_bias_bc"] = rep4(g["dt_bias"][0])
    shared["hmask"] = np.stack([(np.arange(128) < 64), (np.arange(128) >= 64)], 1).astype(f32)
    shared["dn_norm_bc"] = np.ascontiguousarray(np.broadcast_to(g["dn_norm"][0].reshape(1, 128), (128, 128)))
    shared["sink_bc"] = np.ascontiguousarray(np.broadcast_to(g["attn_sink"][0].reshape(1, 8), (128, 8)))
    shared["identf"] = np.eye(128, dtype=f32)
    j = np.arange(128)[:, None]
    i = np.arange(128)[None, :]
    same64 = (i // 64) == (j // 64)
    same32 = (i // 32) == (j // 32)
    dm = np.zeros((128, 6, 128), f32)
    dm[:, 0] = np.where(same64 & (i >= j), 0.0, NEG)
    dm[:, 1] = np.where(same64 & (i <= j), 0.0, NEG)
    dm[:, 2] = (same32 & (i > j))
    dm[:, 3] = (same32 & (i < j))
    dm[:, 4] = (same64 & ~same32 & (i > j))
    dm[:, 5] = (same64 & ~same32 & (i < j))
    shared["dmask"] = dm
    tri = np.zeros((128, 3, 128), f32)
    tri[:, 0] = (same64 & (j <= i))
    tri[:, 1] = (same64 & (j >= i))
    tri[:, 2] = same64
    shared["tri"] = tri
    am = np.zeros((128, 2, 512), f32)
    am[:, 0] = np.tile(np.where(j >= i, 0.0, NEG), (1, 4))
    am[:, 1] = np.tile(np.where(j <= i, 0.0, NEG), (1, 4))
    shared["amask"] = am
    pm = np.eye(128, dtype=f32)
    pm[:32, :32] = 0.0
    for r in range(32):
        pm[(r + 16) % 32, r] = 1.0
    shared["perm32"] = pm
    for n, _, _ in WEIGHTS:
        shared[n] = np.ascontiguousarray(g[n][0])
    half = 16
    inv = np.power(f32(500000.0), -np.arange(half, dtype=f32) / f32(half)).astype(f32)

    def tables(pos):
        ang = pos.astype(f32)[None, :] * inv[:, None]
        c = np.cos(ang).astype(f32)
        sn = np.sin(ang).astype(f32)
        n = pos.shape[0]
        return (np.concatenate([c, c, np.ones((96, n), f32)], 0), np.concatenate([-sn, sn, np.zeros((96, n), f32)], 0))

    cores = []
    for core in range(8):
        if core < 4:
            xx = np.concatenate([xp[2 * core], xp[2 * core + 1]], 0)
            cc = np.stack([cp[2 * core], cp[2 * core + 1]], 0)
            pos = np.concatenate([np.arange(SLOT), np.arange(SLOT)])
            lk = 0.0
        elif core < 6:
            xx = xs[core - 4]
            cc = np.stack([cs[core - 4], cs[core - 4]], 0)
            pos = np.arange(NTOK)
            lk = 1.0
        else:
            xx = np.zeros((NTOK, D), f32)
            cc = np.zeros((2, D), f32)
            pos = np.arange(NTOK)
            lk = 0.0
        ct, st = tables(pos)
        m = dict(shared)
        m["x"] = np.ascontiguousarray(xx, dtype=f32)
        m["cT"] = np.ascontiguousarray(cc.reshape(2, 8, 128).transpose(2, 1, 0), dtype=f32)
        m["link"] = np.full((128, 1), lk, f32)
        m["cosT"] = np.ascontiguousarray(ct)
        m["sinT"] = np.ascontiguousarray(st)
        cores.append(m)
    return cores


def kernel(**inputs):
    cores = host_inputs(inputs)
    nc = build(0)
    res = run_bass_kernel_spmd(nc, cores, core_ids=list(range(8)))
    r = res.results
    y_prompt = np.stack([r[c // 2]["y"][(c % 2) * SLOT:(c % 2 + 1) * SLOT] for c in range(8)], 0).astype(np.float32)
    y_sample = np.stack([r[4]["y"], r[5]["y"]], 0).astype(np.float32)
    return (y_prompt, y_sample)
```

```python
import contextlib
import numpy as np
import concourse.bass as bass
import concourse.mybir as mybir
from concourse.bass_utils import run_bass_kernel_spmd

F32 = mybir.dt.float32
BF16 = mybir.dt.bfloat16
AF = mybir.ActivationFunctionType
ALU = mybir.AluOpType

NTOK = 8192
SLOT = 4096
D = 1024
T = 512
NT = NTOK // T
FF = 2816
PW = 7712
EPS = 1e-6
NEG = -30000.0
O_Z, O_B, O_Q, O_K, O_V, O_GA = 3072, 4096, 4128, 5152, 5408, 5664


class Buf:
    __slots__ = ("name", "lw", "rd", "excl")

    def __init__(self, name, excl=False):
        self.name = name
        self.lw = None
        self.rd = {}
        self.excl = excl


class Sched:
    ENGS = ("pe", "act", "dve", "pool", "sp")

    def __init__(self):
        self.ops = {e: [] for e in self.ENGS}
        self.dq = {}
        self.seen = {}

    def _stream(self, eng):
        if eng in self.ops:
            return self.ops[eng]
        return self.dq.setdefault(eng, [])

    def _deps(self, eng, reads, writes):
        need = {}

        same_ok = eng in ("act", "dve", "pool")

        def add(p, same=False):
            if p is None:
                return
            pe, pi = p
            if pe == eng and not (same and same_ok):
                return
            if need.get(pe, -1) < pi:
                need[pe] = pi

        for b in reads:
            add(b.lw, True)
        for b in writes:
            add(b.lw, True)
            for pe, pi in b.rd.items():
                add((pe, pi))
        return need

    def _commit(self, eng, idx, reads, writes):
        for b in reads:
            b.rd[eng] = idx
        for b in writes:
            b.lw = (eng, idx)
            b.rd = {}

    def _filter(self, cons, need):
        out = {}
        for pe, pi in need.items():
            k = (cons, pe)
            if self.seen.get(k, -1) >= pi:
                continue
            self.seen[k] = pi
            out[pe] = pi
            self._stream(pe)[pi][2] = True
        return out

    def op(self, eng, fn, reads=(), writes=()):
        if any(b.excl for b in reads):
            writes = list(writes) + [b for b in reads if b.excl]
            reads = [b for b in reads if not b.excl]
        need = self._filter(eng, self._deps(eng, reads, writes))
        lst = self.ops[eng]
        idx = len(lst)
        lst.append([fn, need, False, None])
        self._commit(eng, idx, reads, writes)
        return idx

    def dma(self, issuer, cls, fn, reads=(), writes=()):
        rd_dram = bool(reads) and reads[0].name.startswith(("scr_", "inputs"))
        if rd_dram and writes:
            cls = "L_" + writes[0].name
        elif reads:
            cls = "S_" + reads[0].name
            if issuer == "sp":
                issuer = "pool"
        need = self._deps(cls, reads, writes)
        need = {pe: pi for pe, pi in need.items() if pe != issuer}
        need = self._filter(issuer, need)
        q = self._stream(cls)
        didx = len(q)
        q.append([None, {}, True, None])
        self.ops[issuer].append([fn, need, False, (cls, didx)])
        self._commit(cls, didx, reads, writes)
        return didx

    def barrier(self):
        last = {}
        for e in self.ENGS:
            lst = self.ops[e]
            for i in range(len(lst) - 1, -1, -1):
                if lst[i][0] is not None and lst[i][3] is None:
                    last[e] = i
                    break
        for c, q in self.dq.items():
            if q:
                last[c] = len(q) - 1
        for e in self.ENGS:
            need = {p: i for p, i in last.items() if p != e}
            need = self._filter(e, need)
            self.ops[e].append([None, need, False, None])

    def emit(self, nc, sems, mk_sem=None):
        for c in self.dq:
            if c not in sems:
                sems[c] = mk_sem(c)
        num = {}
        for e, lst in list(self.ops.items()) + list(self.dq.items()):
            c = 0
            arr = []
            for o in lst:
                if o[2]:
                    c += 1
                arr.append(c)
            num[e] = arr
        handles = {"pe": "tensor", "act": "scalar", "dve": "vector", "pool": "gpsimd", "sp": "sync"}
        stats = {}
        with nc.Block() as block:
            for e in self.ENGS:
                lst = self.ops[e]
                stats[e] = len(lst)
                if not lst:
                    continue

                def body(engh, e=e, lst=lst):
                    for o in lst:
                        fn, need, needed, dmaref = o
                        for pe, pi in need.items():
                            mult = 16 if pe in self.dq else 1
                            engh.wait_ge(sems[pe], num[pe][pi] * mult)
                        if fn is None:
                            continue
                        ins = fn(engh)
                        if dmaref is not None:
                            ins.then_inc(sems[dmaref[0]], 16)
                        elif needed:
                            ins.then_inc(sems[e], 1)

                getattr(block, handles[e])(body)
        return stats


class Ring:
    def __init__(self, aps, name):
        self.items = [(a, Buf("%s%d" % (name, i))) for i, a in enumerate(aps)]
        self.i = 0

    def next(self):
        it = self.items[self.i % len(self.items)]
        self.i += 1
        return it


class Arena:
    def __init__(self, ap, n):
        self.ap = ap
        self.n = n
        self.off = 0

    def reset(self, off=0):
        self.off = off

    def alloc(self, free, dt):
        free = tuple(free)
        cnt = int(np.prod(free))
        ne = cnt * (2 if dt == F32 else 1)
        assert self.off + ne <= self.n, ("arena overflow", self.off, ne, self.n)
        a = self.ap[:, self.off:self.off + ne]
        self.off += (ne + 31) // 32 * 32
        if dt == F32:
            a = a.bitcast(F32)
        if len(free) == 2:
            a = a.rearrange("p (a b) -> p a b", a=free[0])
        elif len(free) == 3:
            a = a.rearrange("p (a b c) -> p a b c", a=free[0], b=free[1])
        return a


def MM(out, lhsT, rhs, start=True, stop=True):
    return lambda e: e.matmul(out, lhsT=lhsT, rhs=rhs, start=start, stop=stop)


def TR(out, in_, ident):
    return lambda e: e.transpose(out=out, in_=in_, identity=ident)


def ACTV(out, in_, func, bias=None, scale=None, accum_out=None):
    kw = {}
    if bias is not None:
        kw["bias"] = bias
    if scale is not None:
        kw["scale"] = scale
    if accum_out is not None:
        kw["accum_out"] = accum_out
    return lambda e: e.activation(out=out, in_=in_, func=func, **kw)


def TT(out, in0, in1, op):
    return lambda e: e.tensor_tensor(out=out, in0=in0, in1=in1, op=op)


def TS(out, in0, s1, s2, op0, op1=None):
    if op1 is None:
        return lambda e: e.tensor_scalar(out=out, in0=in0, scalar1=s1, scalar2=None, op0=op0)
    return lambda e: e.tensor_scalar(out=out, in0=in0, scalar1=s1, scalar2=s2, op0=op0, op1=op1)


def STT(out, in0, scalar, in1, op0, op1):
    return lambda e: e.scalar_tensor_tensor(out=out, in0=in0, scalar=scalar, in1=in1, op0=op0, op1=op1)


def CP(out, in_):
    return lambda e: e.tensor_copy(out=out, in_=in_)


def MSET(out, v):
    return lambda e: e.memset(out, v)


def DMA(out, in_):
    return lambda e: e.dma_start(out=out, in_=in_)


WEIGHTS = [("ffn1_w_in", D, 2 * FF), ("ffn1_w_out", FF, D), ("w_in", D, PW), ("w_proj_a", D, D),
           ("w_proj_b", D, D), ("w_out", D, D), ("ffn2_w_in", D, 2 * FF), ("ffn2_w_out", FF, D)]


def build(dev=0):
    nc = bass.Bass("TRN2", target_bir_lowering=False)
    S = Sched()

    def din(name, shape, dt=F32):
        return nc.dram_tensor(name, list(shape), dt, kind="ExternalInput").ap()

    def dscr(name, shape, dt):
        return nc.dram_tensor(name, list(shape), dt, kind=("ExternalOutput" if dev else "Internal")).ap()

    x_d = din("x", [NTOK, D])
    cT_d = din("cT", [128, 8, 2])
    link_d = din("link", [128, 1])
    cos_d = din("cosT", [128, NTOK])
    sin_d = din("sinT", [128, NTOK])
    wada_d = din("w_ada", [D, 9 * D])
    bada_d = din("b_adaT", [128, 72])
    nrm_d = din("nrmT", [128, 4, 8])
    convw_d = din("conv_wT", [128, 24, 5])
    alog_d = din("a_log_bc", [128, 64])
    dtb_d = din("dt_bias_bc", [128, 64])
    hmask_d = din("hmask", [128, 2])
    dnn_d = din("dn_norm_bc", [128, 128])
    sink_d = din("sink_bc", [128, 8])
    identf_d = din("identf", [128, 128])
    dmask_d = din("dmask", [128, 6, 128])
    tri_d = din("tri", [128, 3, 128])
    amask_d = din("amask", [128, 2, 512])
    perm_d = din("perm32", [128, 128])
    wsrc = {n: din(n, [k, m]) for n, k, m in WEIGHTS}
    y_d = nc.dram_tensor("y", [NTOK, D], F32, kind="ExternalOutput").ap()

    wb = {n: nc.dram_tensor(n + "_b", [k, m], BF16, kind="Internal").ap() for n, k, m in WEIGHTS}
    x1T_d = dscr("x1T", [D, NTOK], F32)
    qkvT_d = dscr("qkvT", [3072, NTOK], BF16)
    zs_d = dscr("zs", [NTOK, D], BF16)
    ba_d = dscr("ba", [NTOK, 32], F32)
    aqT_d = dscr("aqT", [D, NTOK], BF16)
    akT_d = dscr("akT", [256, NTOK], BF16)
    avs_d = dscr("avs", [NTOK, 256], BF16)
    gT_d = dscr("gT", [2048, NTOK], BF16)
    obs_d = dscr("obs", [NTOK, D], F32)
    dnT_d = dscr("dnT", [D, NTOK], BF16)
    atT_d = dscr("atT", [D, NTOK], BF16)
    modT_o = dscr("modT_o", [128, 144], F32) if dev else None
    hdbg = dscr("hdbg", [128, 8 * 512], BF16) if dev else None
    hiddbg = dscr("hiddbg", [128, 24 * 512], BF16) if dev else None
    ofs_d = dscr("ofs", [NTOK, D], F32) if dev else None
    gdbg = dscr("gdbg", [128, 128], F32) if dev else None

    AR_N = 93 * 1024
    with contextlib.ExitStack() as es:
        def sbt(name, shape, dt):
            return es.enter_context(nc.sbuf_tensor(name, list(shape), dt))

        arena_t = sbt("arena", [128, AR_N], BF16)
        ar = Arena(arena_t[:, :], AR_N)
        identf = sbt("identf_s", [128, 128], F32)
        identb = sbt("identb_s", [128, 128], BF16)
        onesb = sbt("onesb_s", [128, 128], BF16)
        link = sbt("link_s", [128, 1], F32)
        epsb = sbt("epsb_s", [128, 1], F32)
        modT = sbt("modT_s", [128, 72, 2], F32)
        A_t = sbt("A_s", [128, 3, 2, 8], F32)
        G_t = sbt("G_s", [128, 3, 2, 8], F32)
        AF_t = sbt("AF_s", [128, 8], F32)
        nrmT = sbt("nrmT_s", [128, 4, 8], F32)
        cT = sbt("cT_s", [128, 8, 2], F32)
        badaT = sbt("badaT_s", [128, 72], F32)
        psum = [es.enter_context(nc.psum_tensor("ps%d" % i, [128, 512], F32)) for i in range(8)]
        semn = ["pe", "act", "dve", "pool", "sp", "sp_ld", "sp_ldw", "sp_st"]
        sems = {n: es.enter_context(nc.semaphore(n)) for n in semn}
        bconst = Buf("const")
        bmod = Buf("mod")
        bscr = {n: Buf("scr_" + n) for n in ["wb", "x1T", "qkvT", "zs", "ba", "aqT", "akT", "avs", "gT", "obs", "dnT", "atT", "y", "dbg"]}
        bin_ = Buf("inputs")

        def psring():
            return Ring([p[:, :] for p in psum], "ps")

        S.dma("sp", "sp_ld", DMA(identf[:, :], identf_d), reads=[bin_], writes=[bconst])
        S.dma("sp", "sp_ld", DMA(link[:, :], link_d), reads=[bin_], writes=[bconst])
        S.dma("sp", "sp_ld", DMA(nrmT[:, :, :], nrm_d), reads=[bin_], writes=[bconst])
        S.dma("sp", "sp_ld", DMA(cT[:, :, :], cT_d), reads=[bin_], writes=[bconst])
        S.dma("sp", "sp_ld", DMA(badaT[:, :], bada_d), reads=[bin_], writes=[bconst])
        S.op("dve", CP(identb[:, :], identf[:, :]), reads=[bconst], writes=[bconst])
        S.op("dve", MSET(onesb[:, :], 1.0), writes=[bconst])
        S.op("dve", MSET(epsb[:, :], 1024.0 * EPS), writes=[bconst])

        ar.reset()
        stg = Ring([ar.alloc((2816,), F32) for _ in range(3)], "stg")
        cvt = Ring([ar.alloc((2816,), BF16) for _ in range(3)], "cvt")
        engs = ["act", "dve"]
        ei = 0
        for n, K, N in (WEIGHTS if dev != 12 else []):
            for r in range(K // 128):
                for c0 in range(0, N, 2816):
                    cw = min(2816, N - c0)
                    st, stb = stg.next()
                    cv, cvb = cvt.next()
                    S.dma("sp", "sp_ldw", DMA(st[:, :cw], wsrc[n][r * 128:(r + 1) * 128, c0:c0 + cw]), reads=[bin_], writes=[stb])
                    e = engs[ei % 2]
                    ei += 1
                    if e == "act":
                        S.op("act", ACTV(cv[:, :cw], st[:, :cw], AF.Copy), reads=[stb], writes=[cvb])
                    else:
                        S.op(e, CP(cv[:, :cw], st[:, :cw]), reads=[stb], writes=[cvb])
                    S.dma("sp", "sp_st", DMA(wb[n][r * 128:(r + 1) * 128, c0:c0 + cw], cv[:, :cw]), reads=[cvb], writes=[bscr["wb"]])

        scT = ar.alloc((8, 2), F32)
        bsc = Buf("scT")
        S.op("act", ACTV(scT, cT[:, :, :], AF.Silu), reads=[bconst], writes=[bsc])
        wring = Ring([ar.alloc((8, 1152), F32) for _ in range(2)], "wada")
        pm = psum[0][:, 0:144]
        pmb = Buf("pm")
        wada_v = wada_d.rearrange("(kc p) n -> p kc n", p=128)
        for jb in (range(8) if dev != 13 else []):
            wt, wtb = wring.next()
            S.dma("sp", "sp_ldw", DMA(wt, wada_v[:, :, jb * 1152:(jb + 1) * 1152]), reads=[bin_], writes=[wtb])
            for jj in range(9):
                j = jb * 9 + jj
                for kc in range(8):
                    S.op("pe", MM(pm[:, 2 * j:2 * j + 2], wt[:, kc, jj * 128:(jj + 1) * 128], scT[:, kc, :], kc == 0, kc == 7),
                         reads=[wtb, bsc], writes=[pmb])
        pm3 = pm.rearrange("p (j s) -> p j s", s=2)
        for s in range(2):
            S.op("dve", TT(modT[:, :, s], pm3[:, :, s], badaT[:, :], ALU.add), reads=[pmb, bconst], writes=[bmod])
        for n in range(3):
            for s in range(2):
                S.op("dve", STT(A_t[:, n, s, :], modT[:, (3 * n + 1) * 8:(3 * n + 1) * 8 + 8, s], 1.0, nrmT[:, n, :], ALU.add, ALU.mult),
                     reads=[bmod, bconst], writes=[bmod])
                S.op("dve", TS(A_t[:, n, s, :], A_t[:, n, s, :], 32.0, None, ALU.mult), reads=[bmod], writes=[bmod])
                S.op("dve", TS(G_t[:, n, s, :], modT[:, (3 * n + 2) * 8:(3 * n + 2) * 8 + 8, s], (1.0 if n == 1 else 0.5), None, ALU.mult),
                     reads=[bmod], writes=[bmod])
        S.op("dve", TS(AF_t[:, :], nrmT[:, 3, :], 32.0, None, ALU.mult), reads=[bconst], writes=[bmod])
        if dev:
            S.dma("sp", "sp_st", DMA(modT_o, modT[:, :, :].rearrange("p j s -> p (j s)")), reads=[bmod], writes=[bscr["dbg"]])
        S.barrier()

        def finish():
            S.dma("sp", "sp_st", DMA(y_d[0:128, 0:128], identf[:, :]), reads=[bconst], writes=[bscr["y"]])
            S.barrier()
            st = S.emit(nc, sems, lambda n: es.enter_context(nc.semaphore(n)))
            print("instr counts", st)
            return nc

        if dev in (11, 12, 13):
            return finish()

        def rms_ada(xT, xb, n, s, hout, hb, rings, out_f32=False):
            sqr, rsr, tmr, psr = rings
            ps, psb = psr.next()
            for kc in range(8):
                sq, sqb = sqr.next()
                S.op("act", ACTV(sq, xT[:, kc, :], AF.Square), reads=[xb], writes=[sqb])
                S.op("pe", MM(ps, onesb[:, :], sq, kc == 0, kc == 7), reads=[sqb, bconst], writes=[psb])
            rs, rsb = rsr.next()
            S.op("act", ACTV(rs, ps, AF.Sqrt, bias=epsb[:, 0:1], scale=1.0), reads=[psb, bconst], writes=[rsb])
            S.op("dve", lambda e, rs=rs: e.reciprocal(out=rs, in_=rs), reads=[rsb], writes=[rsb])
            for kc in range(8):
                if n == 3:
                    S.op("dve", STT(hout[:, kc, :], xT[:, kc, :], AF_t[:, kc:kc + 1], rs, ALU.mult, ALU.mult),
                         reads=[xb, rsb, bmod], writes=[hb])
                else:
                    tm, tmb = tmr.next()
                    S.op("dve", STT(tm, xT[:, kc, :], A_t[:, n, s, kc:kc + 1], rs, ALU.mult, ALU.mult),
                         reads=[xb, rsb, bmod], writes=[tmb])
                    S.op("act", ACTV(hout[:, kc, :], tm, AF.Identity, bias=modT[:, 3 * n * 8 + kc, s:s + 1], scale=1.0),
                         reads=[tmb, bmod], writes=[hb])

        def gemm_fm(Wd, KC, segs_per_group, rhs, rhs_bufs, epi, wr, psr):
            Wv = Wd.rearrange("(kc p) n -> p kc n", p=128)
            for g, segs in enumerate(segs_per_group):
                wt, wtb = wr.next()
                gw = sum(w for _, w in segs)
                wt = wt[:, 0:KC * gw].rearrange("p (kc n) -> p kc n", kc=KC)
                o = 0
                for c0, w in segs:
                    S.dma("sp", "sp_ldw", DMA(wt[:, :, o:o + w], Wv[:, :, c0:c0 + w]), reads=[bscr["wb"]], writes=[wtb])
                    o += w
                outs = []
                for ci in range(o // 128):
                    ps, psb = psr.next()
                    for kc in range(KC):
                        S.op("pe", MM(ps, wt[:, kc, ci * 128:(ci + 1) * 128], rhs(kc), kc == 0, kc == KC - 1),
                             reads=[wtb] + rhs_bufs, writes=[psb])
                    outs.append((ps, psb))
                epi(g, outs)

        def gemm_tm(Wd, KC, c0, ncols, lhsT, lhs_bufs, epi, wr, psr):
            Wv = Wd.rearrange("(kc p) n -> p kc n", p=128)
            wt, wtb = wr.next()
            wt = wt[:, 0:KC * ncols].rearrange("p (kc n) -> p kc n", kc=KC)
            S.dma("sp", "sp_ldw", DMA(wt, Wv[:, :, c0:c0 + ncols]), reads=[bscr["wb"]], writes=[wtb])
            for tb in range(4):
                ps, psb = psr.next()
                for kc in range(KC):
                    S.op("pe", MM(ps[:, 0:ncols], lhsT(kc, tb), wt[:, kc, :], kc == 0, kc == KC - 1),
                         reads=[wtb] + lhs_bufs, writes=[psb])
                epi(tb, ps[:, 0:ncols], psb)

        def swiglu_ffn(w_in_name, w_out_name, h, hb, hid, hidb, xT, xb, n, s, wr, psr, sgr):
            segs = [[(j * 128, 256), (FF + j * 128, 256)] for j in range(0, 22, 2)]

            def epi_in(g, outs):
                for q in range(2):
                    sg, sgb = sgr.next()
                    S.op("act", ACTV(sg, outs[q][0], AF.Silu), reads=[outs[q][1]], writes=[sgb])
                    S.op("dve", TT(hid[:, g * 2 + q, :], sg, outs[2 + q][0], ALU.mult), reads=[sgb, outs[2 + q][1]], writes=[hidb])

            gemm_fm(wb[w_in_name], 8, segs, lambda kc: h[:, kc, :], [hb], epi_in, wr, psr)
            segs2 = [[(fo * 256, 256)] for fo in range(4)]

            def epi_out(g, outs):
                for q in range(2):
                    fo = g * 2 + q
                    S.op("dve", STT(xT[:, fo, :], outs[q][0], G_t[:, n, s, fo:fo + 1], xT[:, fo, :], ALU.mult, ALU.add),
                         reads=[outs[q][1], bmod, xb], writes=[xb])

            gemm_fm(wb[w_out_name], 22, segs2, lambda j: hid[:, j, :], [hidb], epi_out, wr, psr)

        ar.reset()
        xT = ar.alloc((8, 512), F32); xTb = Buf("xT")
        h = ar.alloc((8, 512), BF16); hb = Buf("h")
        R1 = ar.alloc((24, 512), BF16); bR1 = Buf("R1")
        hid = R1
        R2 = ar.alloc((16, 512), BF16); bR2 = Buf("R2")
        xin = R2.rearrange("p a b -> p (a b)").bitcast(F32).rearrange("p (a b) -> p a b", a=4)
        aq_st = ar.alloc((10, 512), BF16); baq = Buf("aqst")
        z_st = ar.alloc((4, 1024), BF16); bz = Buf("zst")
        ba_st = ar.alloc((4, 32), F32); bba = Buf("bast")
        v_st = ar.alloc((4, 256), BF16); bv = Buf("vst")
        cs_t = ar.alloc((2, 512), F32); bcs = Buf("cs")
        xr_r = Ring([ar.alloc((512,), F32) for _ in range(2)], "xr")
        rt_r = Ring([ar.alloc((512,), F32) for _ in range(2)], "rt")
        sqr = Ring([ar.alloc((512,), BF16) for _ in range(2)], "sq")
        rsr = Ring([ar.alloc((512,), F32) for _ in range(2)], "rs")
        tmr = Ring([ar.alloc((512,), F32) for _ in range(3)], "tm")
        sgr = Ring([ar.alloc((512,), F32) for _ in range(3)], "sg")
        wr = Ring([ar.alloc((22 * 256,), BF16) for _ in range(3)], "w")
        perm = ar.alloc((128,), F32); bperm = Buf("perm")
        S.dma("sp", "sp_ld", DMA(perm[:, :], perm_d), reads=[bin_], writes=[bperm])
        psr = psring()
        rings = (sqr, rsr, tmr, psr)
        n_tiles = NT if dev not in (10, 14, 15, 16, 17, 18, 19) else 2
        for ti in range(n_tiles):
            t0 = ti * T
            s = ti // (NT // 2)
            S.dma("sp", "sp_ld", DMA(xin, x_d[t0:t0 + T, :].rearrange("(tb p) f -> p tb f", p=128)), reads=[bin_], writes=[bR2])
            for kc in range(8):
                ps, psb = psr.next()
                for tb in range(4):
                    S.op("pe", TR(ps[:, tb * 128:(tb + 1) * 128], xin[:, tb, kc * 128:(kc + 1) * 128], identf[:, :]),
                         reads=[bR2, bconst], writes=[psb])
                S.op(("act" if kc % 2 else "dve"), (ACTV(xT[:, kc, :], ps, AF.Copy) if kc % 2 else CP(xT[:, kc, :], ps)),
                     reads=[psb], writes=[xTb])
            if dev != 14:
                rms_ada(xT, xTb, 0, s, h, hb, rings)
            if dev not in (14, 15):
                swiglu_ffn("ffn1_w_in", "ffn1_w_out", h, hb, hid, bR1, xT, xTb, 0, s, wr, psr, sgr)
            S.dma("sp", "sp_st", DMA(x1T_d.rearrange("(kc p) t -> p kc t", p=128)[:, :, t0:t0 + T], xT), reads=[xTb], writes=[bscr["x1T"]])
            if dev == 16 and ti == 0:
                S.dma("sp", "sp_st", DMA(hdbg, h.rearrange("p a b -> p (a b)")), reads=[hb], writes=[bscr["dbg"]])
                S.dma("sp", "sp_st", DMA(hiddbg, R1.rearrange("p a b -> p (a b)")), reads=[bR1], writes=[bscr["dbg"]])
            if dev in (14, 15, 16):
                continue
            rms_ada(xT, xTb, 1, s, h, hb, rings)
            hk = lambda kc: h[:, kc, :]
            def epi_qkv(g, outs):
                for q, (ps, psb) in enumerate(outs):
                    c = g * 4 + q
                    if q % 2:
                        S.op("act", ACTV(R1[:, c, :], ps, AF.Copy), reads=[psb], writes=[bR1])
                    else:
                        S.op("dve", CP(R1[:, c, :], ps), reads=[psb], writes=[bR1])
            gemm_fm(wb["w_in"], 8, [[(g * 512, 512)] for g in range(6)], hk, [hb], epi_qkv, wr, psr)
            for c8 in range(3):
                S.dma("sp", "sp_st", DMA(qkvT_d.rearrange("(c p) t -> p c t", p=128)[:, c8 * 8:(c8 + 1) * 8, t0:t0 + T], R1[:, c8 * 8:(c8 + 1) * 8, :]), reads=[bR1], writes=[bscr["qkvT"]])
            if dev == 17:
                continue
            S.dma("sp", "sp_ld", DMA(cs_t[:, 0, :], cos_d[:, t0:t0 + T]), reads=[bin_], writes=[bcs])
            S.dma("sp", "sp_ld", DMA(cs_t[:, 1, :], sin_d[:, t0:t0 + T]), reads=[bin_], writes=[bcs])

            def epi_rope(base):
                def f(g, outs):
                    for q, (ps, psb) in enumerate(outs):
                        c = base + g * 4 + q
                        xr, xrb = xr_r.next()
                        S.op("dve", CP(xr[:, :], ps[:, :]), reads=[psb], writes=[xrb])
                        pp, ppb = psr.next()
                        S.op("pe", MM(pp[:, :], perm[:, :], xr[:, :]), reads=[bperm, xrb], writes=[ppb])
                        rt, rtb = rt_r.next()
                        S.op("dve", TT(rt[:, :], pp[:, :], cs_t[:, 1, :], ALU.mult), reads=[ppb, bcs], writes=[rtb])
                        S.op("dve", TT(xr[:, :], xr[:, :], cs_t[:, 0, :], ALU.mult), reads=[xrb, bcs], writes=[xrb])
                        S.op("dve", TT(aq_st[:, c, :], xr[:, :], rt[:, :], ALU.add), reads=[xrb, rtb, baq], writes=[baq])
                return f
            gemm_fm(wb["w_in"], 8, [[(O_Q + g * 512, 512)] for g in range(2)], hk, [hb], epi_rope(0), wr, psr)
            gemm_fm(wb["w_in"], 8, [[(O_K, 256)]], hk, [hb], epi_rope(8), wr, psr)
            S.dma("sp", "sp_st", DMA(aqT_d.rearrange("(c p) t -> p c t", p=128)[:, :, t0:t0 + T], aq_st[:, 0:8, :]), reads=[baq], writes=[bscr["aqT"]])
            S.dma("sp", "sp_st", DMA(akT_d.rearrange("(c p) t -> p c t", p=128)[:, :, t0:t0 + T], aq_st[:, 8:10, :]), reads=[baq], writes=[bscr["akT"]])
            if dev == 18:
                continue
            def epi_gate(g, outs):
                for q, (ps, psb) in enumerate(outs):
                    S.op("act", ACTV(R2[:, g * 4 + q, :], ps, AF.Sigmoid), reads=[psb], writes=[bR2])
            gemm_fm(wb["w_in"], 8, [[(O_GA + g * 512, 512)] for g in range(4)], hk, [hb], epi_gate, wr, psr)
            for c8 in range(2):
                S.dma("sp", "sp_st", DMA(gT_d.rearrange("(c p) t -> p c t", p=128)[:, c8 * 8:(c8 + 1) * 8, t0:t0 + T], R2[:, c8 * 8:(c8 + 1) * 8, :]), reads=[bR2], writes=[bscr["gT"]])
            if dev == 19:
                continue
            hl = lambda kc, tb: h[:, kc, tb * 128:(tb + 1) * 128]
            for zc in range(2):
                def epi_z(tb, ps, psb, zc=zc):
                    S.op("act", ACTV(z_st[:, tb, zc * 512:(zc + 1) * 512], ps, AF.Silu), reads=[psb], writes=[bz])
                gemm_tm(wb["w_in"], 8, O_Z + zc * 512, 512, hl, [hb], epi_z, wr, psr)
            S.dma("sp", "sp_st", DMA(zs_d[t0:t0 + T, :].rearrange("(tb p) f -> p tb f", p=128), z_st), reads=[bz], writes=[bscr["zs"]])

            def epi_ba(tb, ps, psb):
                S.op("dve", CP(ba_st[:, tb, :], ps), reads=[psb], writes=[bba])
            gemm_tm(wb["w_in"], 8, O_B, 32, hl, [hb], epi_ba, wr, psr)
            S.dma("sp", "sp_st", DMA(ba_d[t0:t0 + T, :].rearrange("(tb p) f -> p tb f", p=128), ba_st), reads=[bba], writes=[bscr["ba"]])

            def epi_v(tb, ps, psb):
                S.op("dve", CP(v_st[:, tb, :], ps), reads=[psb], writes=[bv])
            gemm_tm(wb["w_in"], 8, O_V, 256, hl, [hb], epi_v, wr, psr)
            S.dma("sp", "sp_st", DMA(avs_d[t0:t0 + T, :].rearrange("(tb p) f -> p tb f", p=128), v_st), reads=[bv], writes=[bscr["avs"]])
        S.barrier()

        if dev in (1, 10, 14, 15, 16, 17, 18, 19):
            return finish()

        ar.reset()
        es_t = ar.alloc((8,), F32); bes = Buf("es")
        sk_t = ar.alloc((8,), F32)
        am_t = ar.alloc((2, 512), F32); bam = Buf("am")
        S.dma("sp", "sp_ld", DMA(sk_t, sink_d), reads=[bin_], writes=[bes])
        S.dma("sp", "sp_ld", DMA(am_t, amask_d), reads=[bin_], writes=[bam])
        S.op("act", ACTV(es_t, sk_t, AF.Exp), reads=[bes], writes=[bes])
        q_r = Ring([ar.alloc((8, 512), BF16) for _ in range(2)], "aq")
        k_r = Ring([ar.alloc((2, 768), BF16) for _ in range(2)], "akk")
        v_r = Ring([ar.alloc((6, 256), BF16) for _ in range(2)], "avv")
        at_r = Ring([ar.alloc((8, 512), BF16) for _ in range(2)], "ato")
        pt_r = Ring([ar.alloc((512,), BF16) for _ in range(6)], "pt")
        sc_r = Ring([ar.alloc((512,), F32) for _ in range(3)], "sc")
        rd_r = Ring([ar.alloc((512,), F32) for _ in range(2)], "rd")
        zero_t = ar.alloc((512,), BF16); bzero = Buf("zero")
        S.op("pool", MSET(zero_t, 0.0), writes=[bzero])
        psr = psring()
        SCALE = 128.0 ** -0.5
        aqv = aqT_d.rearrange("(c p) t -> p c t", p=128)
        akv = akT_d.rearrange("(c p) t -> p c t", p=128)
        for ti in range(NT):
            t0 = ti * T
            qt, qtb = q_r.next(); kt, ktb = k_r.next(); vt, vtb = v_r.next(); at, atb = at_r.next()
            S.dma("sp", "sp_ld", DMA(qt, aqv[:, :, t0:t0 + T]), reads=[bscr["aqT"]], writes=[qtb])
            lo = max(t0 - 128, 0); hi = min(t0 + T + 128, NTOK)
            S.dma("sp", "sp_ld", DMA(kt[:, :, lo - (t0 - 128):hi - (t0 - 128)], akv[:, :, lo:hi]), reads=[bscr["akT"]], writes=[ktb])
            nb0 = (lo - (t0 - 128)) // 128; nb1 = (hi - (t0 - 128)) // 128
            S.dma("sp", "sp_ld", DMA(vt[:, nb0:nb1, :], avs_d[lo:hi, :].rearrange("(kb p) f -> p kb f", p=128)), reads=[bscr["avs"]], writes=[vtb])
            for bl in range(4):
                gb = ti * 4 + bl
                for g in range(2):
                    pts = []
                    for dk_ in (-1, 0, 1):
                        kb = gb + dk_
                        if kb < 0 or kb > 63:
                            continue
                        kl = bl + 1 + dk_
                        ps, psb = psr.next()
                        for hh in range(4):
                            S.op("pe", MM(ps[:, hh * 128:(hh + 1) * 128], kt[:, g, kl * 128:(kl + 1) * 128], qt[:, 4 * g + hh, bl * 128:(bl + 1) * 128]),
                                 reads=[ktb, qtb], writes=[psb])
                        pt, ptb = pt_r.next()
                        if dk_ == 0:
                            S.op("act", ACTV(pt, ps, AF.Exp, scale=SCALE), reads=[psb], writes=[ptb])
                        else:
                            sc, scb = sc_r.next()
                            S.op("dve", STT(sc, ps, SCALE, am_t[:, (0 if dk_ < 0 else 1), :], ALU.mult, ALU.add), reads=[psb, bam], writes=[scb])
                            S.op("act", ACTV(pt, sc, AF.Exp), reads=[scb], writes=[ptb])
                            if (kb // 32) != (gb // 32):
                                S.op("pool", TS(pt, pt, link[:, 0:1], None, ALU.mult), reads=[ptb, bconst], writes=[ptb])
                        pts.append((pt, ptb, kl))
                    po, pob = psr.next()
                    pd, pdb = psr.next()
                    for i_, (pt, ptb, kl) in enumerate(pts):
                        S.op("pe", MM(po, vt[:, kl, g * 128:(g + 1) * 128], pt, i_ == 0, i_ == len(pts) - 1), reads=[vtb, ptb], writes=[pob])
                    for i_, (pt, ptb, kl) in enumerate(pts):
                        S.op("pe", MM(pd, onesb[:, :], pt, i_ == 0, i_ == len(pts) - 1), reads=[bconst, ptb], writes=[pdb])
                    rd, rdb = rd_r.next()
                    for hh in range(4):
                        S.op("dve", TS(rd[:, hh * 128:(hh + 1) * 128], pd[:, hh * 128:(hh + 1) * 128], es_t[:, 4 * g + hh:4 * g + hh + 1], None, ALU.add),
                             reads=[pdb, bes], writes=[rdb])
                    S.op("dve", lambda e, rd=rd: e.reciprocal(out=rd, in_=rd), reads=[rdb], writes=[rdb])
                    S.op("dve", TT(at[:, 4 * g:4 * g + 4, bl * 128:(bl + 1) * 128], po.rearrange("p (h q) -> p h q", h=4),
                                   rd.rearrange("p (h q) -> p h q", h=4), ALU.mult), reads=[pob, rdb], writes=[atb])
            S.dma("sp", "sp_st", DMA(atT_d.rearrange("(c p) t -> p c t", p=128)[:, :, t0:t0 + T], at), reads=[atb], writes=[bscr["atT"]])
        S.barrier()
        if dev == 2:
            return finish()

        ar.reset()
        bk = Buf("dconst")
        convw = ar.alloc((24, 5), F32)
        dmk = ar.alloc((6, 128), F32)
        trit = ar.alloc((3, 128), F32)
        negb = ar.alloc((2, 128), BF16)
        hmk = ar.alloc((2,), F32)
        alog = ar.alloc((2, 4, 8), F32)
        dtb = ar.alloc((2, 4, 8), F32)
        negA = ar.alloc((2, 4, 8), F32)
        dnn = ar.alloc((128,), F32)
        onesf = ar.alloc((1,), F32)
        eps1 = ar.alloc((1,), F32)
        diagW = ar.alloc((24, 5, 128), BF16)
        for dst, src in ((convw, convw_d), (dmk, dmask_d), (trit, tri_d), (hmk, hmask_d), (dnn, dnn_d)):
            S.dma("sp", "sp_ld", DMA(dst, src), reads=[bin_], writes=[bk])
        S.dma("sp", "sp_ld", DMA(alog.rearrange("p a b c -> p (a b c)"), alog_d), reads=[bin_], writes=[bk])
        S.dma("sp", "sp_ld", DMA(dtb.rearrange("p a b c -> p (a b c)"), dtb_d), reads=[bin_], writes=[bk])
        S.op("act", ACTV(negA, alog, AF.Exp), reads=[bk], writes=[bk])
        S.op("dve", TS(negA, negA, -1.0, None, ALU.mult), reads=[bk], writes=[bk])
        S.op("dve", CP(negb, dmk[:, 0:2, :]), reads=[bk], writes=[bk])
        S.op("dve", MSET(onesf, 1.0), writes=[bk])
        S.op("dve", MSET(eps1, EPS), writes=[bk])
        bdw = Buf("diagW")
        for c in range(24):
            for j in range(5):
                S.op(("dve" if (c * 5 + j) % 2 else "pool"), TS(diagW[:, c, j, :], identf[:, :], convw[:, c, j:j + 1], None, ALU.mult),
                     reads=[bconst, bk], writes=[bdw])

        bankb = [Buf("bank%d" % i, excl=True) for i in range(8)]

        class PRing(Ring):
            def __init__(self, items, name):
                self.items = items
                self.i = 0

        def hq(h, k):
            return psum[h][:, k * 128:(k + 1) * 128], bankb[h]

        def hh2(h, k):
            return psum[h][:, k * 256:(k + 1) * 256], bankb[h]

        Pr = PRing([(psum[i][:, :], bankb[i]) for i in range(8)], "pp")
        F32r = PRing([(psum[0][:, q * 128:(q + 1) * 128], bankb[0]) for q in range(4)], "p32")
        Qr = PRing([(psum[1 + i][:, q * 128:(q + 1) * 128], bankb[1 + i]) for i in range(3) for q in range(4)], "pq")
        Hr = PRing([(psum[4 + i][:, hh * 256:(hh + 1) * 256], bankb[4 + i]) for i in range(2) for hh in range(2)], "ph")
        Fr = PRing([(psum[6 + i][:, :], bankb[6 + i]) for i in range(2)], "pf")

        def tl(shape, dt, name):
            return (ar.alloc(shape, dt), Buf(name))

        qkt, qkb = tl((24, 516), BF16, "qkt")
        bat, bab = tl((4, 32), F32, "bat")
        oacc, boa = tl((4, 8, 128), F32, "oacc")
        R3 = ar.alloc((4, 8, 128), F32); bR3 = Buf("R3")
        ob_t = R3
        R3h = R3.rearrange("p a b c -> p (a b c)").bitcast(BF16)
        dn_t = R3h[:, 0:4096].rearrange("p (a b) -> p a b", a=4)
        dst_t = R3h[:, 4096:8192].rearrange("p (a b) -> p a b", a=8)
        zs_t, bzs = tl((4, 1024), BF16, "zst")
        kt_all, bkt = tl((8, 128), F32, "kt")
        sqk, bsqk = tl((8, 128), F32, "sqk")
        sqo, bsqo = sqk, bsqk
        v_all, bva = tl((8, 128), BF16, "va")
        qT2 = [tl((8, 128), BF16, "qa%d" % i) for i in range(2)]
        qsq, bqs = tl((8, 128), BF16, "qsq")
        kn_all, bkn = tl((8, 128), BF16, "kn")
        knT_all, bknT = tl((8, 128), BF16, "knT")
        bet, bbet = tl((4, 8), F32, "bet")
        g_t_, bgg = tl((4, 8), F32, "g")
        ta_, bta = tl((4, 8), F32, "ta")
        gc_, bgc = tl((4, 8), F32, "gc")
        ngc, bngc = tl((4, 8), F32, "ngc")
        egc, begc = tl((4, 8), F32, "egc")
        kds, bkds = tl((4, 8), F32, "kds")
        glb, bglb = tl((4, 8, 2), F32, "glb")
        ghm, bghm = tl((4, 8, 2), F32, "ghm")
        nbh, bnbh = tl((4, 8, 2), F32, "nbh")
        bh_, bbh = tl((4, 8, 2), F32, "bh")
        ssk, bssk = tl((8,), F32, "ssk")
        cq, bcq = tl((8,), F32, "cq")
        cqh2 = [tl((8, 2), F32, "cqh%d" % i) for i in range(2)]
        ecq2 = [tl((8, 2), F32, "ecq%d" % i) for i in range(2)]
        sso, bsso = tl((32,), F32, "sso")
        def ph(n, shape, dt, name):
            return [tl(shape, dt, "%s%d" % (name, i)) for i in range(n)]
        dec = ph(8, (128,), F32, "dec"); LT = ph(8, (128,), F32, "LT")
        gtri = LT
        ones32 = ar.alloc((128,), F32)
        S.op("dve", MSET(ones32, 1.0), writes=[bk])
        LdT = ph(8, (128,), BF16, "LdT"); LoT = ph(8, (128,), BF16, "LoT"); Ld = ph(8, (128,), BF16, "Ld")
        PRa = ph(8, (256,), BF16, "PRa"); PRb = ph(8, (256,), BF16, "PRb")
        Pa = ph(8, (128,), BF16, "Pa"); Pb = ph(8, (128,), BF16, "Pb")
        XT = ph(8, (128,), BF16, "XT"); ke = ph(8, (128,), BF16, "ke")
        yy = ph(8, (256,), BF16, "yy"); rr_ = ph(8, (256,), BF16, "rr")
        ub = ph(8, (128,), BF16, "ub"); t2 = dec
        u0b = ph(8, (2, 128), F32, "u0b"); wT = ph(8, (128,), BF16, "wT")
        qkT = ph(8, (128,), BF16, "qkT"); kdec = ph(8, (128,), BF16, "kdec")
        Sst = ph(8, (128,), F32, "S"); Sb_ = ph(8, (128,), BF16, "Sb")
        qkvv = qkvT_d.rearrange("(c p) t -> p c t", p=128)
        X = mybir.AxisListType.X
        print("phase2b arena use (bf16 elems)", ar.off, "of", ar.n)

        def bf(ps):
            return ps.bitcast(BF16)

        class _Stop(Exception):
            pass

        def ck(n):
            if dev == n:
                raise _Stop()

        try:
          ck(22)
          for d in (1, 0):
            for h in range(8):
                S.op("pool", MSET(Sst[h][0], 0.0), writes=[Sst[h][1]])
                S.op("pool", MSET(Sb_[h][0], 0.0), writes=[Sb_[h][1]])
            tiles = list(range(NT)) if d == 0 else list(range(NT - 1, -1, -1))
            if dev == 21:
                tiles = tiles[:2]
            for ti in tiles:
                t0 = ti * T
                if (d == 0 and ti == NT // 2) or (d == 1 and ti == NT // 2 - 1):
                    for h in range(8):
                        S.op("dve", TS(Sst[h][0], Sst[h][0], link[:, 0:1], None, ALU.mult), reads=[Sst[h][1], bconst], writes=[Sst[h][1]])
                        S.op("act", ACTV(Sb_[h][0], Sst[h][0], AF.Copy), reads=[Sst[h][1]], writes=[Sb_[h][1]])
                lo = max(t0 - 2, 0); hi = min(t0 + T + 2, NTOK)
                for c8 in range(3):
                    S.dma("sp", "sp_ld", DMA(qkt[:, c8 * 8:(c8 + 1) * 8, lo - (t0 - 2):hi - (t0 - 2)], qkvv[:, c8 * 8:(c8 + 1) * 8, lo:hi]),
                          reads=[bscr["qkvT"]], writes=[qkb])
                if ti == 0:
                    S.op("pool", MSET(qkt[:, :, 0:2], 0.0), writes=[qkb])
                if ti == NT - 1:
                    S.op("pool", MSET(qkt[:, :, 514:516], 0.0), writes=[qkb])
                if ti == NT // 2 - 1:
                    S.op("pool", TS(qkt[:, :, 514:516], qkt[:, :, 514:516], link[:, 0:1], None, ALU.mult), reads=[qkb, bconst], writes=[qkb])
                if ti == NT // 2:
                    S.op("pool", TS(qkt[:, :, 0:2], qkt[:, :, 0:2], link[:, 0:1], None, ALU.mult), reads=[qkb, bconst], writes=[qkb])
                S.dma("sp", "sp_ld", DMA(bat, ba_d[t0:t0 + T, :].rearrange("(bl p) f -> p bl f", p=128)), reads=[bscr["ba"]], writes=[bab])
                ck(231)
                S.op("act", ACTV(bet, bat[:, :, d * 8:(d + 1) * 8], AF.Sigmoid), reads=[bab], writes=[bbet])
                S.op("dve", TT(ta_, bat[:, :, 16 + d * 8:16 + (d + 1) * 8], dtb[:, d, :, :], ALU.add), reads=[bab, bk], writes=[bta])
                S.op("act", ACTV(ta_, ta_, AF.Exp), reads=[bta], writes=[bta])
                S.op("act", ACTV(ta_, ta_, AF.Ln, bias=onesf[:, 0:1], scale=1.0), reads=[bta, bk], writes=[bta])
                S.op("dve", TT(g_t_, ta_, negA[:, d, :, :], ALU.mult), reads=[bta, bk], writes=[bgg])
                ck(232)
                _pb, _pbb = Pr.next(); pg1, pg1b = _pb[:, 0:128], _pbb; pg2, pg2b = _pb[:, 128:256], _pbb; pg3, pg3b = _pb[:, 256:384], _pbb
                for bl in range(4):
                    S.op("pe", MM(pg1[:, bl * 8:(bl + 1) * 8], trit[:, d, :], g_t_[:, bl, :]), reads=[bk, bgg], writes=[pg1b])
                    S.op("pe", MM(pg2[:, bl * 8:(bl + 1) * 8], trit[:, 2, :], g_t_[:, bl, :]), reads=[bk, bgg], writes=[pg2b])

                for hf in range(2):
                    S.op("dve", TS(ghm[:, :, :, hf], g_t_, hmk[:, hf:hf + 1], None, ALU.mult), reads=[bgg, bk], writes=[bghm])
                S.op("pe", MM(pg3[:, 0:64], ones32, ghm.rearrange("p a b c -> p (a b c)")), reads=[bghm, bk], writes=[pg3b])
                ck(233)
                fl = lambda a: a.rearrange("p a b -> p (a b)")
                if dev != 2343:
                    S.op("dve", CP(fl(gc_), pg1[:, 0:32]), reads=[pg1b], writes=[bgc])
                ck(2341)
                if dev != 2343:
                    S.op("dve", TS(fl(ngc), pg1[:, 0:32], -1.0, None, ALU.mult), reads=[pg1b], writes=[bngc])
                if dev == 2342:
                    S.dma("sp", "sp_st", DMA(gdbg[:, 0:32], fl(g_t_)), reads=[bgg], writes=[bscr["dbg"]])
                    S.dma("sp", "sp_st", DMA(gdbg[:, 32:64], fl(gc_)), reads=[bgc], writes=[bscr["dbg"]])
                    S.dma("sp", "sp_st", DMA(gdbg[:, 64:96], fl(ngc)), reads=[bngc], writes=[bscr["dbg"]])
                    S.dma("sp", "sp_st", DMA(gdbg[:, 96:128], fl(bet)), reads=[bbet], writes=[bscr["dbg"]])
                ck(2342)
                S.op("act", ACTV(fl(egc), fl(gc_), AF.Exp), reads=[bgc], writes=[begc])
                ck(2343)
                ck(234)
                S.op("dve", TT(fl(kds), pg2[:, 0:32], fl(gc_), ALU.subtract), reads=[pg2b, bgc], writes=[bkds])
                S.op("act", ACTV(fl(kds), fl(kds), AF.Exp), reads=[bkds], writes=[bkds])
                ck(235)
                S.op("dve", CP(glb.rearrange("p a b c -> p (a b c)"), pg3[:, 0:64]), reads=[pg3b], writes=[bglb])
                S.op("act", ACTV(glb.rearrange("p a b c -> p (a b c)"), glb.rearrange("p a b c -> p (a b c)"), AF.Exp), reads=[bglb], writes=[bglb])
                ck(236)
                for hf in range(2):
                    S.op("dve", TS(nbh[:, :, :, hf], bet, hmk[:, hf:hf + 1], -1.0, ALU.mult, ALU.mult), reads=[bbet, bk], writes=[bnbh])
                    S.op("dve", TS(bh_[:, :, :, hf], bet, hmk[:, hf:hf + 1], None, ALU.mult), reads=[bbet, bk], writes=[bbh])

                ck(23)
                def prep_chunks(bl, par):
                    c0 = bl * 128
                    qT_all, bqa = qT2[par]
                    cqh, bcqh = cqh2[par]
                    ecq, becq = ecq2[par]
                    out = []
                    for (cb, dstt, dstb, tokmaj) in ((8, kt_all, bkt, True), (16, v_all, bva, True), (0, qT_all, bqa, False)):
                        for hg in range(2):
                            def conv(cb=cb, dstt=dstt, dstb=dstb, tokmaj=tokmaj, hg=hg):
                                ps, psb = Pr.next()
                                for hh in range(4):
                                    h = hg * 4 + hh
                                    for j in range(5):
                                        if tokmaj:
                                            S.op("pe", MM(ps[:, hh * 128:(hh + 1) * 128], qkt[:, cb + h, c0 + j:c0 + j + 128], diagW[:, cb + h, j, :], j == 0, j == 4),
                                                 reads=[qkb, bdw], writes=[psb])
                                        else:
                                            S.op("pe", MM(ps[:, hh * 128:(hh + 1) * 128], diagW[:, cb + h, j, :], qkt[:, cb + h, c0 + j:c0 + j + 128], j == 0, j == 4),
                                                 reads=[qkb, bdw], writes=[psb])
                                S.op("act", ACTV(dstt[:, hg * 4:(hg + 1) * 4, :].rearrange("p a b -> p (a b)"), ps, AF.Silu), reads=[psb], writes=[dstb])
                            out.append(conv)

                    def knorm():
                        S.op("act", ACTV(sqk, kt_all, AF.Square), reads=[bkt], writes=[bsqk])
                        S.op("dve", lambda e: e.tensor_reduce(out=ssk, in_=sqk, axis=X, op=ALU.add), reads=[bsqk], writes=[bssk])
                        S.op("act", ACTV(ssk, ssk, AF.Sqrt, bias=eps1[:, 0:1], scale=1.0), reads=[bssk, bk], writes=[bssk])
                        S.op("dve", lambda e: e.reciprocal(out=ssk, in_=ssk), reads=[bssk], writes=[bssk])
                        for h in range(8):
                            S.op("dve", TS(kn_all[:, h, :], kt_all[:, h, :], ssk[:, h:h + 1], None, ALU.mult), reads=[bkt, bssk], writes=[bkn])

                    def ktr():
                        ps, psb = Pr.next()
                        for h in range(8):
                            S.op("pe", TR(bf(ps)[:, h * 128:(h + 1) * 128], kn_all[:, h, :], identb[:, :]), reads=[bkn, bconst], writes=[psb])
                        S.op("act", ACTV(knT_all.rearrange("p a b -> p (a b)"), bf(ps)[:, 0:1024], AF.Copy), reads=[psb], writes=[bknT])

                    def qnorm():
                        S.op("act", ACTV(qsq, qT_all, AF.Square), reads=[bqa], writes=[bqs])
                        ps, psb = Pr.next()
                        for h in range(8):
                            S.op("pe", MM(ps[:, h:h + 1], qsq[:, h, :], onesb[:, 0:1]), reads=[bqs, bconst], writes=[psb])
                        S.op("act", ACTV(cq, ps[:, 0:8], AF.Sqrt, bias=eps1[:, 0:1], scale=1.0), reads=[psb, bk], writes=[bcq])
                        S.op("dve", lambda e: e.reciprocal(out=cq, in_=cq), reads=[bcq], writes=[bcq])
                        for hf in range(2):
                            S.op("dve", TS(cqh[:, :, hf], cq, hmk[:, hf:hf + 1], 128.0 ** -0.5, ALU.mult, ALU.mult), reads=[bcq, bk], writes=[bcqh])
                            S.op("dve", TT(ecq[:, :, hf], cqh[:, :, hf], egc[:, bl, :], ALU.mult), reads=[bcqh, begc], writes=[becq])
                    out += [knorm, ktr, qnorm]
                    return out

                order = list(range(4)) if d == 0 else [3, 2, 1, 0]
                for c_ in prep_chunks(order[0], 0):
                    c_()
                for kk_, bl in enumerate(order):
                    qT_all, bqa = qT2[kk_ % 2]
                    cqh, bcqh = cqh2[kk_ % 2]
                    ecq, becq = ecq2[kk_ % 2]
                    nxt = prep_chunks(order[kk_ + 1], (kk_ + 1) % 2) if kk_ + 1 < 4 else []
                    ck(24)
                    for hg in range(1):
                        HS = list(range(8))
                        cur = {}

                        def st_a(h):
                            i = h
                            pe_, peb = hq(h, 0); pg, pgb = hq(h, 1); pq, pqb = hq(h, 2)
                            S.op("act", ACTV(gtri[i][0], trit[:, d, :], AF.Copy, scale=g_t_[:, bl, h:h + 1]), reads=[bgg, bk], writes=[gtri[i][1]])
                            S.op("pe", MM(pe_, ones32, gtri[i][0]), reads=[gtri[i][1], bk], writes=[peb])
                            S.op("pe", MM(pg, knT_all[:, h, :], knT_all[:, h, :]), reads=[bknT], writes=[pgb])
                            S.op("pe", MM(pq, knT_all[:, h, :], qT_all[:, h, :]), reads=[bknT, bqa], writes=[pqb])
                            cur[h] = dict(pe=(pe_, peb), pg=(pg, pgb), pq=(pq, pqb))

                        def st_b1(h):
                            i = h; c = cur[h]
                            S.op("dve", STT(dec[i][0], c["pe"][0], ngc[:, bl, h:h + 1], dmk[:, d, :], ALU.add, ALU.add), reads=[c["pe"][1], bk, bngc], writes=[dec[i][1]])

                        def st_b2(h):
                            i = h
                            S.op("act", ACTV(dec[i][0], dec[i][0], AF.Exp), reads=[dec[i][1]], writes=[dec[i][1]])

                        def st_b3(h):
                            i = h; c = cur[h]
                            S.op("dve", STT(LT[i][0], c["pg"][0], bet[:, bl, h:h + 1], dec[i][0], ALU.mult, ALU.mult), reads=[c["pg"][1], bbet, dec[i][1]], writes=[LT[i][1]])
                            S.op("dve", TT(qkT[h][0], c["pq"][0], dec[i][0], ALU.mult), reads=[c["pq"][1], dec[i][1]], writes=[qkT[h][1]])

                        def st_b4(h):
                            i = h
                            S.op("dve", TT(LdT[i][0], LT[i][0], dmk[:, 2 + d, :], ALU.mult), reads=[LT[i][1], bk], writes=[LdT[i][1]])
                            S.op("dve", TT(LoT[i][0], LT[i][0], dmk[:, 4 + d, :], ALU.mult), reads=[LT[i][1], bk], writes=[LoT[i][1]])
                            S.op("dve", TT(PRa[i][0][:, 128:256], identb[:, :], LdT[i][0], ALU.subtract), reads=[bconst, LdT[i][1]], writes=[PRa[i][1]])

                        def st_c(h):
                            i = h
                            pt, ptb = hq(h, 3)
                            S.op("pe", TR(bf(pt)[:, 0:128], LdT[i][0], identb[:, :]), reads=[LdT[i][1], bconst], writes=[ptb])
                            cur[h]["pt"] = (pt, ptb)

                        def st_d(h):
                            i = h
                            S.op("act", ACTV(Ld[i][0], bf(cur[h]["pt"][0])[:, 0:128], AF.Copy), reads=[cur[h]["pt"][1]], writes=[Ld[i][1]])

                        def st_e(h):
                            i = h
                            p1t, p1tb = hq(h, 0); p1, p1b = hq(h, 1)
                            if dev == 2553:
                                S.op("pe", MM(p1t, LdT[i][0], LdT[i][0]), reads=[Ld[i][1], LdT[i][1]], writes=[p1tb])
                            elif dev != 2552:
                                S.op("pe", MM(p1t, Ld[i][0], LdT[i][0]), reads=[Ld[i][1], LdT[i][1]], writes=[p1tb])
                            if dev not in (2551, 2553):
                                S.op("pe", MM(p1, LdT[i][0], Ld[i][0]), reads=[Ld[i][1], LdT[i][1]], writes=[p1b])
                            cur[h]["a"] = (p1t, p1tb); cur[h]["b"] = (p1, p1b)

                        def st_f(h):
                            i = h
                            S.op("act", ACTV(PRa[i][0][:, 0:128], cur[h]["a"][0], AF.Copy), reads=[cur[h]["a"][1]], writes=[PRa[i][1]])
                            S.op("act", ACTV(Pa[i][0], cur[h]["b"][0], AF.Copy), reads=[cur[h]["b"][1]], writes=[Pa[i][1]])

                        def mk_lvl(PRs, Ps, PRd, Pd, hsel, qsel):
                            def mm(h):
                                i = h
                                pA, pAb = hh2(h, hsel); pB, pBb = hq(h, qsel)
                                S.op("pe", MM(pA, Ps[i][0], PRs[i][0]), reads=[Ps[i][1], PRs[i][1]], writes=[pAb])
                                S.op("pe", MM(pB, PRs[i][0][:, 0:128], Ps[i][0]), reads=[Ps[i][1], PRs[i][1]], writes=[pBb])
                                cur[h]["a"] = (pA, pAb); cur[h]["b"] = (pB, pBb)

                            def ev(h):
                                i = h
                                pA, pAb = cur[h]["a"]; pB, pBb = cur[h]["b"]
                                S.op("act", ACTV(PRd[i][0][:, 0:128], pA[:, 0:128], AF.Copy), reads=[pAb], writes=[PRd[i][1]])
                                S.op("act", ACTV(Pd[i][0], pB, AF.Copy), reads=[pBb], writes=[Pd[i][1]])
                                S.op("dve", TT(PRd[i][0][:, 128:256], PRs[i][0][:, 128:256], pA[:, 128:256], ALU.add), reads=[pAb, PRs[i][1]], writes=[PRd[i][1]])
                            return mm, ev

                        l1m, l1e = mk_lvl(PRa, Pa, PRb, Pb, 1, 0)
                        l2m, l2e = mk_lvl(PRb, Pb, PRa, Pa, 0, 2)

                        def st_g(h):
                            i = h
                            pA, pAb = hq(h, 0); pB, pBb = hq(h, 1)
                            S.op("pe", MM(pA, Pa[i][0], PRa[i][0][:, 128:256]), reads=[Pa[i][1], PRa[i][1]], writes=[pAb])
                            S.op("pe", MM(pB, PRa[i][0][:, 0:128], Pa[i][0]), reads=[Pa[i][1], PRa[i][1]], writes=[pBb])
                            cur[h]["a"] = (pA, pAb); cur[h]["b"] = (pB, pBb)

                        def st_h(h):
                            i = h
                            S.op("act", ACTV(Pb[i][0], cur[h]["b"][0], AF.Copy), reads=[cur[h]["b"][1]], writes=[Pb[i][1]])
                            S.op("dve", TT(PRb[i][0][:, 128:256], PRa[i][0][:, 128:256], cur[h]["a"][0], ALU.add), reads=[cur[h]["a"][1], PRa[i][1]], writes=[PRb[i][1]])

                        def st_i(h):
                            i = h
                            px, pxb = hq(h, 2)
                            S.op("pe", MM(px, Pb[i][0], PRb[i][0][:, 128:256]), reads=[Pb[i][1], PRb[i][1]], writes=[pxb])
                            cur[h]["a"] = (px, pxb)

                        def st_j(h):
                            i = h
                            S.op("dve", TT(XT[i][0], PRb[i][0][:, 128:256], cur[h]["a"][0], ALU.add), reads=[cur[h]["a"][1], PRb[i][1]], writes=[XT[i][1]])
                            S.op("act", ACTV(ke[i][0], kn_all[:, h, :], AF.Copy, scale=egc[:, bl, h:h + 1]), reads=[bkn, begc], writes=[ke[i][1]])
                            S.op("act", ACTV(kdec[h][0], kn_all[:, h, :], AF.Copy, scale=kds[:, bl, h:h + 1]), reads=[bkn, bkds], writes=[kdec[h][1]])

                        def st_k(h):
                            i = h
                            py, pyb = hh2(h, 1)
                            S.op("pe", MM(py[:, 0:128], XT[i][0], v_all[:, h, :]), reads=[XT[i][1], bva], writes=[pyb])
                            S.op("pe", MM(py[:, 128:256], XT[i][0], ke[i][0]), reads=[XT[i][1], ke[i][1]], writes=[pyb])
                            cur[h]["a"] = (py, pyb)

                        def st_l(h):
                            i = h
                            S.op("act", ACTV(yy[i][0], cur[h]["a"][0], AF.Copy), reads=[cur[h]["a"][1]], writes=[yy[i][1]])

                        def st_m(h):
                            i = h
                            pz, pzb = hh2(h, 0)
                            S.op("pe", MM(pz, LoT[i][0], yy[i][0]), reads=[LoT[i][1], yy[i][1]], writes=[pzb])
                            cur[h]["a"] = (pz, pzb)

                        def st_n(h):
                            i = h
                            pz, pzb = cur[h]["a"]
                            S.op("dve", TT(rr_[i][0][:, 0:128], v_all[:, h, :], pz[:, 0:128], ALU.subtract), reads=[pzb, bva], writes=[rr_[i][1]])
                            S.op("dve", TT(rr_[i][0][:, 128:256], ke[i][0], pz[:, 128:256], ALU.subtract), reads=[pzb, ke[i][1]], writes=[rr_[i][1]])

                        def st_o(h):
                            i = h
                            pu, pub = hq(h, 2); pw, pwb = hq(h, 3)
                            S.op("pe", MM(pu, XT[i][0], rr_[i][0][:, 0:128]), reads=[XT[i][1], rr_[i][1]], writes=[pub])
                            S.op("pe", MM(pw, rr_[i][0][:, 128:256], XT[i][0]), reads=[XT[i][1], rr_[i][1]], writes=[pwb])
                            cur[h]["a"] = (pu, pub); cur[h]["b"] = (pw, pwb)

                        def st_p(h):
                            pu, pub = cur[h]["a"]; pw, pwb = cur[h]["b"]
                            S.op("act", ACTV(u0b[h][0][:, 0, :], pu, AF.Copy, scale=bh_[:, bl, h, 0:1]), reads=[pub, bbh], writes=[u0b[h][1]])
                            S.op("act", ACTV(u0b[h][0][:, 1, :], pu, AF.Copy, scale=bh_[:, bl, h, 1:2]), reads=[pub, bbh], writes=[u0b[h][1]])
                            S.op("act", ACTV(wT[h][0], pw, AF.Copy), reads=[pwb], writes=[wT[h][1]])

                        safe_pts = {}
                        stages = [st_a, st_b1, st_b2, st_b3, st_b4, st_c, st_d, st_e, st_f, l1m, l1e, l2m, l2e, st_g, st_h, st_i, st_j, st_k, st_l, st_m, st_n, st_o, st_p]
                        for k_, hf in enumerate((0, 1) if d == 0 else (1, 0)):
                            def sc_a(h):
                                pws, pwsb = hq(h, 0)
                                S.op("pe", MM(pws, wT[h][0], Sb_[h][0]), reads=[wT[h][1], Sb_[h][1]], writes=[pwsb])
                                cur[h]["a"] = (pws, pwsb)

                            def sc_b(h, hf=hf):
                                i = h
                                S.op("dve", STT(ub[i][0], cur[h]["a"][0], nbh[:, bl, h, hf:hf + 1], u0b[h][0][:, hf, :], ALU.mult, ALU.add),
                                     reads=[cur[h]["a"][1], bnbh, u0b[h][1]], writes=[ub[i][1]])

                            def sc_c(h):
                                i = h
                                p1_, p1b_ = hq(h, 1); p2_, p2b_ = hq(h, 2); pS, pSb = hq(h, 3)
                                S.op("pe", MM(p1_, qkT[h][0], ub[i][0]), reads=[qkT[h][1], ub[i][1]], writes=[p1b_])
                                S.op("pe", MM(p2_, qT_all[:, h, :], Sb_[h][0]), reads=[bqa, Sb_[h][1]], writes=[p2b_])
                                S.op("pe", MM(pS, kdec[h][0], ub[i][0]), reads=[kdec[h][1], ub[i][1]], writes=[pSb])
                                cur[h]["o1"] = (p1_, p1b_); cur[h]["o2"] = (p2_, p2b_); cur[h]["s"] = (pS, pSb)

                            def sc_d1(h, hf=hf):
                                i = h
                                S.op("act", ACTV(t2[i][0], cur[h]["o2"][0], AF.Copy, scale=ecq[:, h, hf:hf + 1]), reads=[cur[h]["o2"][1], becq], writes=[t2[i][1]])

                            def sc_d2(h, hf=hf, k_=k_):
                                i = h
                                S.op("dve", STT(Sst[h][0], Sst[h][0], glb[:, bl, h, hf:hf + 1], cur[h]["s"][0], ALU.mult, ALU.add),
                                     reads=[Sst[h][1], bglb, cur[h]["s"][1]], writes=[Sst[h][1]])
                                if k_ == 0:
                                    S.op("dve", STT(oacc[:, bl, h, :], cur[h]["o1"][0], cqh[:, h, hf:hf + 1], t2[i][0], ALU.mult, ALU.add),
                                         reads=[cur[h]["o1"][1], bcqh, t2[i][1]], writes=[boa])
                                else:
                                    S.op("dve", STT(t2[i][0], cur[h]["o1"][0], cqh[:, h, hf:hf + 1], t2[i][0], ALU.mult, ALU.add),
                                         reads=[cur[h]["o1"][1], bcqh, t2[i][1]], writes=[t2[i][1]])

                            def sc_d3(h, k_=k_):
                                i = h
                                S.op("act", ACTV(Sb_[h][0], Sst[h][0], AF.Copy), reads=[Sst[h][1]], writes=[Sb_[h][1]])
                                if k_ == 1:
                                    S.op("dve", TT(oacc[:, bl, h, :], oacc[:, bl, h, :], t2[i][0], ALU.add), reads=[boa, t2[i][1]], writes=[boa])
                            stages += [sc_a, sc_b, sc_c, sc_d1, sc_d2, sc_d3]
                            safe_pts[sc_b] = 2
                            safe_pts[sc_d3] = 3
                        if 250 <= dev < 280:
                            stages = stages[:dev - 250]
                        if dev in (2521, 2522, 2523, 2524, 2525, 2526):
                            stages = stages[:2]
                        if dev in (2551, 2552, 2553):
                            stages = stages[:5]
                        if dev == 25:
                            stages = stages[:6]
                        if dev == 26:
                            stages = stages[:14]
                        if dev == 27:
                            stages = stages[:20]
                        for f in stages:
                            for h in HS:
                                f(h)
                            if f in safe_pts:
                                for _ in range(safe_pts[f]):
                                    if nxt:
                                        nxt.pop(0)()
                        while nxt:
                            nxt.pop(0)()
                        if dev == 254 and hg == 0:
                            for i in range(4):
                                S.dma("sp", "sp_st", DMA(hdbg[:, i * 256:i * 256 + 128], LdT[i][0]), reads=[LdT[i][1]], writes=[bscr["dbg"]])
                                S.dma("sp", "sp_st", DMA(hdbg[:, i * 256 + 128:i * 256 + 256], Ld[i][0]), reads=[Ld[i][1]], writes=[bscr["dbg"]])
                            S.dma("sp", "sp_st", DMA(gdbg[:, :], dec[0][0]), reads=[dec[0][1]], writes=[bscr["dbg"]])
                        if dev in (25, 26, 27, 2521, 2522, 2523, 2524, 2525, 2526, 2551, 2552, 2553) or 250 <= dev < 280:
                            raise _Stop()
                if d == 1:
                    S.dma("sp", "sp_st", DMA(obs_d[t0:t0 + T, :].rearrange("(bl p) f -> p bl f", p=128), oacc.rearrange("p a b c -> p a (b c)")),
                          reads=[boa], writes=[bscr["obs"]])
                else:
                    S.dma("sp", "sp_ld", DMA(ob_t.rearrange("p a b c -> p a (b c)"), obs_d[t0:t0 + T, :].rearrange("(bl p) f -> p bl f", p=128)),
                          reads=[bscr["obs"]], writes=[bR3])
                    S.dma("sp", "sp_ld", DMA(zs_t, zs_d[t0:t0 + T, :].rearrange("(bl p) f -> p bl f", p=128)), reads=[bscr["zs"]], writes=[bzs])
                    if dev:
                        S.dma("sp", "sp_st", DMA(ofs_d[t0:t0 + T, :].rearrange("(bl p) f -> p bl f", p=128), oacc.rearrange("p a b c -> p a (b c)")),
                              reads=[boa], writes=[bscr["dbg"]])
                    oflat = oacc.rearrange("p a b c -> p (a b c)")
                    S.op("dve", TT(oflat, oflat, ob_t.rearrange("p a b c -> p (a b c)"), ALU.add), reads=[boa, bR3], writes=[boa])
                    for bl in range(4):
                        S.op("pool", TT(sqo, oacc[:, bl, :, :], oacc[:, bl, :, :], ALU.mult), reads=[boa], writes=[bsqo])
                        S.op("dve", lambda e, bl=bl: e.tensor_reduce(out=sso[:, bl * 8:(bl + 1) * 8], in_=sqo, axis=X, op=ALU.add), reads=[bsqo], writes=[bsso])
                    S.op("act", ACTV(sso, sso, AF.Sqrt, bias=eps1[:, 0:1], scale=1.0 / 128.0), reads=[bsso, bk], writes=[bsso])
                    S.op("dve", lambda e: e.reciprocal(out=sso, in_=sso), reads=[bsso], writes=[bsso])
                    for bl in range(4):
                        for h in range(8):
                            S.op("dve", STT(oacc[:, bl, h, :], oacc[:, bl, h, :], sso[:, bl * 8 + h:bl * 8 + h + 1], dnn, ALU.mult, ALU.mult),
                                 reads=[boa, bsso, bk], writes=[boa])
                    S.op("dve", TT(dn_t.rearrange("p a b -> p (a b)"), oflat, zs_t.rearrange("p a b -> p (a b)"), ALU.mult), reads=[boa, bzs, bR3], writes=[bR3])
                    for bl in range(4):
                        ps, psb = Pr.next()
                        for h in range(8):
                            S.op("pe", TR(bf(ps)[:, h * 128:(h + 1) * 128], dn_t[:, bl, h * 128:(h + 1) * 128], identb[:, :]), reads=[bR3, bconst], writes=[psb])
                        S.op("act", ACTV(dst_t[:, :, bl * 128:(bl + 1) * 128], bf(ps)[:, 0:1024].rearrange("p (a b) -> p a b", a=8), AF.Copy), reads=[psb, bR3], writes=[bR3])
                    S.dma("sp", "sp_st", DMA(dnT_d.rearrange("(c p) t -> p c t", p=128)[:, :, t0:t0 + T], dst_t), reads=[bR3], writes=[bscr["dnT"]])
            S.barrier()
        except _Stop:
            S.barrier()
            return finish()
        if dev in (3, 21):
            return finish()

        ar.reset()
        xT = ar.alloc((8, 512), F32); xTb = Buf("xT3")
        h = ar.alloc((8, 512), BF16); hb = Buf("h3")
        R1 = ar.alloc((24, 512), BF16); bR1 = Buf("R13")
        hid = R1
        dn_t = ar.alloc((8, 512), BF16); bdn = Buf("dnt")
        at_t = ar.alloc((8, 512), BF16); bat = Buf("att")
        g_t = ar.alloc((16, 512), BF16); bg = Buf("gt")
        yout = g_t.rearrange("p a b -> p (a b)").bitcast(F32).rearrange("p (a b) -> p a b", a=4)
        tmpA = ar.alloc((8, 512), F32); btA = Buf("tmpA")
        yT = tmpA
        mrg = ar.alloc((8, 512), BF16); bmr = Buf("mrg")
        tb_r = Ring([ar.alloc((512,), F32) for _ in range(2)], "tmpB")
        sqr = Ring([ar.alloc((512,), BF16) for _ in range(2)], "sq3")
        rsr = Ring([ar.alloc((512,), F32) for _ in range(2)], "rs3")
        tmr = Ring([ar.alloc((512,), F32) for _ in range(3)], "tm3")
        sgr = Ring([ar.alloc((512,), F32) for _ in range(3)], "sg3")
        wr = Ring([ar.alloc((22 * 256,), BF16) for _ in range(3)], "w3")
        psr = psring()
        rings = (sqr, rsr, tmr, psr)
        for ti in range(NT):
            t0 = ti * T
            s = ti // (NT // 2)
            S.dma("sp", "sp_ld", DMA(xT, x1T_d.rearrange("(kc p) t -> p kc t", p=128)[:, :, t0:t0 + T]), reads=[bscr["x1T"]], writes=[xTb])
            S.dma("sp", "sp_ld", DMA(dn_t, dnT_d.rearrange("(c p) t -> p c t", p=128)[:, :, t0:t0 + T]), reads=[bscr["dnT"]], writes=[bdn])
            S.dma("sp", "sp_ld", DMA(at_t, atT_d.rearrange("(c p) t -> p c t", p=128)[:, :, t0:t0 + T]), reads=[bscr["atT"]], writes=[bat])
            for c8 in range(2):
                S.dma("sp", "sp_ld", DMA(g_t[:, c8 * 8:(c8 + 1) * 8, :], gT_d.rearrange("(c p) t -> p c t", p=128)[:, c8 * 8:(c8 + 1) * 8, t0:t0 + T]),
                      reads=[bscr["gT"]], writes=[bg])

            def epi_a(g, outs):
                for q, (ps, psb) in enumerate(outs):
                    fo = g * 4 + q
                    S.op("dve", TT(tmpA[:, fo, :], ps, g_t[:, fo, :], ALU.mult), reads=[psb, bg], writes=[btA])
            gemm_fm(wb["w_proj_a"], 8, [[(g * 512, 512)] for g in range(2)], lambda kc: dn_t[:, kc, :], [bdn], epi_a, wr, psr)

            def epi_b(g, outs):
                for q, (ps, psb) in enumerate(outs):
                    fo = g * 4 + q
                    tb_, tbb = tb_r.next()
                    S.op("dve", TT(tb_, ps, g_t[:, 8 + fo, :], ALU.mult), reads=[psb, bg], writes=[tbb])
                    S.op("dve", TT(mrg[:, fo, :], tmpA[:, fo, :], tb_, ALU.add), reads=[btA, tbb], writes=[bmr])
            gemm_fm(wb["w_proj_b"], 8, [[(g * 512, 512)] for g in range(2)], lambda kc: at_t[:, kc, :], [bat], epi_b, wr, psr)

            def epi_o(g, outs, s=s):
                for q, (ps, psb) in enumerate(outs):
                    fo = g * 4 + q
                    S.op("dve", STT(xT[:, fo, :], ps, G_t[:, 1, s, fo:fo + 1], xT[:, fo, :], ALU.mult, ALU.add), reads=[psb, bmod, xTb], writes=[xTb])
            gemm_fm(wb["w_out"], 8, [[(g * 512, 512)] for g in range(2)], lambda kc: mrg[:, kc, :], [bmr], epi_o, wr, psr)
            rms_ada(xT, xTb, 2, s, h, hb, rings)
            swiglu_ffn("ffn2_w_in", "ffn2_w_out", h, hb, hid, bR1, xT, xTb, 2, s, wr, psr, sgr)
            rms_ada(xT, xTb, 3, s, yT, btA, rings)
            for tb in range(4):
                for k2 in range(2):
                    ps, psb = psr.next()
                    for kq in range(4):
                        kc = k2 * 4 + kq
                        S.op("pe", TR(ps[:, kq * 128:(kq + 1) * 128], yT[:, kc, tb * 128:(tb + 1) * 128], identf[:, :]), reads=[btA, bconst], writes=[psb])
                    if k2:
                        S.op("act", ACTV(yout[:, tb, k2 * 512:(k2 + 1) * 512], ps, AF.Copy), reads=[psb], writes=[bg])
                    else:
                        S.op("dve", CP(yout[:, tb, k2 * 512:(k2 + 1) * 512], ps), reads=[psb], writes=[bg])
            S.dma("sp", "sp_st", DMA(y_d[t0:t0 + T, :].rearrange("(tb p) f -> p tb f", p=128), yout), reads=[bg], writes=[bscr["y"]])
        S.barrier()
        st = S.emit(nc, sems, lambda n: es.enter_context(nc.semaphore(n)))
        print("instr counts", st)
    return nc


def host_inputs(inputs):
    f32 = np.float32
    g = {k: np.asarray(v) for k, v in inputs.items()}
    xp, xs = g["x_prompt"], g["x_sample"]
    cp, cs = g["c_prompt"], g["c_sample"]
    shared = {}
    shared["w_ada"] = np.ascontiguousarray(g["w_ada"][0])
    shared["b_adaT"] = np.ascontiguousarray(g["b_ada"][0].reshape(72, 128).T)
    nrm = np.stack([g["ffn1_norm"][0], g["mix_norm"][0], g["ffn2_norm"][0], g["final_norm"]], 0)
    shared["nrmT"] = np.ascontiguousarray(nrm.reshape(4, 8, 128).transpose(2, 0, 1))
    shared["conv_wT"] = np.ascontiguousarray(g["conv_w"][0].reshape(5, 24, 128).transpose(2, 1, 0))
    rep4 = lambda a: np.ascontiguousarray(np.broadcast_to(a.reshape(1, 2, 1, 8), (128, 2, 4, 8)).reshape(128, 64))
    shared["a_log_bc"] = rep4(g["a_log"][0])
    shared["dt_bias_bc"] = rep4(g["dt_bias"][0])
    shared["hmask"] = np.stack([(np.arange(128) < 64), (np.arange(128) >= 64)], 1).astype(f32)
    shared["dn_norm_bc"] = np.ascontiguousarray(np.broadcast_to(g["dn_norm"][0].reshape(1, 128), (128, 128)))
    shared["sink_bc"] = np.ascontiguousarray(np.broadcast_to(g["attn_sink"][0].reshape(1, 8), (128, 8)))
    shared["identf"] = np.eye(128, dtype=f32)
    j = np.arange(128)[:, None]
    i = np.arange(128)[None, :]
    same64 = (i // 64) == (j // 64)
    same32 = (i // 32) == (j // 32)
    dm = np.zeros((128, 6, 128), f32)
    dm[:, 0] = np.where(same64 & (i >= j), 0.0, NEG)
    dm[:, 1] = np.where(same64 & (i <= j), 0.0, NEG)
    dm[:, 2] = (same32 & (i > j))
    dm[:, 3] = (same32 & (i < j))
    dm[:, 4] = (same64 & ~same32 & (i > j))
    dm[:, 5] = (same64 & ~same32 & (i < j))
    shared["dmask"] = dm
    tri = np.zeros((128, 3, 128), f32)
    tri[:, 0] = (same64 & (j <= i))
    tri[:, 1] = (same64 & (j >= i))
    tri[:, 2] = same64
    shared["tri"] = tri
    am = np.zeros((128, 2, 512), f32)
    am[:, 0] = np.tile(np.where(j >= i, 0.0, NEG), (1, 4))
    am[:, 1] = np.tile(np.where(j <= i, 0.0, NEG), (1, 4))
    shared["amask"] = am
    pm = np.eye(128, dtype=f32)
    pm[:32, :32] = 0.0
    for r in range(32):
        pm[(r + 16) % 32, r] = 1.0
    shared["perm32"] = pm
    for n, _, _ in WEIGHTS:
        shared[n] = np.ascontiguousarray(g[n][0])
    half = 16
    inv = np.power(f32(500000.0), -np.arange(half, dtype=f32) / f32(half)).astype(f32)

    def tables(pos):
        ang = pos.astype(f32)[None, :] * inv[:, None]
        c = np.cos(ang).astype(f32)
        sn = np.sin(ang).astype(f32)
        n = pos.shape[0]
        return (np.concatenate([c, c, np.ones((96, n), f32)], 0), np.concatenate([-sn, sn, np.zeros((96, n), f32)], 0))

    cores = []
    for core in range(8):
        if core < 4:
            xx = np.concatenate([xp[2 * core], xp[2 * core + 1]], 0)
            cc = np.stack([cp[2 * core], cp[2 * core + 1]], 0)
            pos = np.concatenate([np.arange(SLOT), np.arange(SLOT)])
            lk = 0.0
        elif core < 6:
            xx = xs[core - 4]
            cc = np.stack([cs[core - 4], cs[core - 4]], 0)
            pos = np.arange(NTOK)
            lk = 1.0
        else:
            xx = np.zeros((NTOK, D), f32)
            cc = np.zeros((2, D), f32)
            pos = np.arange(NTOK)
            lk = 0.0
        ct, st = tables(pos)
        m = dict(shared)
        m["x"] = np.ascontiguousarray(xx, dtype=f32)
        m["cT"] = np.ascontiguousarray(cc.reshape(2, 8, 128).transpose(2, 1, 0), dtype=f32)
        m["link"] = np.full((128, 1), lk, f32)
        m["cosT"] = np.ascontiguousarray(ct)
        m["sinT"] = np.ascontiguousarray(st)
        cores.append(m)
    return cores


def kernel(**inputs):
    cores = host_inputs(inputs)
    nc = build(0)
    res = run_bass_kernel_spmd(nc, cores, core_ids=list(range(8)))
    r = res.results
    y_prompt = np.stack([r[c // 2]["y"][(c % 2) * SLOT:(c % 2 + 1) * SLOT] for c in range(8)], 0).astype(np.float32)
    y_sample = np.stack([r[4]["y"], r[5]["y"]], 0).astype(np.float32)
    return (y_prompt, y_sample)
```
